# Optimizing a Trainium2 kernel written in Bass

```python
import math
import jax, jax.numpy as jnp
from jax import lax
import numpy as np

D_MODEL = 1024
BATCH = 1
SEQ = 16384
DEPTH = 1

ATTN_WIDTH = D_MODEL // 2
CONV_WIDTH = D_MODEL - ATTN_WIDTH
DA_HEAD_DIM = 64
DA_V_DIM = 2 * DA_HEAD_DIM
DA_HEADS = ATTN_WIDTH // DA_V_DIM
CONV_K = 31
FFN_CONV_K = 3
D_FF = 2816
ROPE_THETA = 10000.0
Q_BLOCK = 128
EPS = 1e-6
IN_COLS = 3 * ATTN_WIDTH + 2 * CONV_WIDTH

kernel_name = "hybrid_diffattn_conformer_convffn_sandwich"


def rms_norm(x, g):
    xf = x.astype(jnp.float32)
    y = xf * lax.rsqrt(jnp.mean(xf * xf, axis=-1, keepdims=True) + EPS)
    return (y * g.astype(jnp.float32)).astype(x.dtype)


def layer_norm(x, g, b):
    xf = x.astype(jnp.float32)
    mu = jnp.mean(xf, axis=-1, keepdims=True)
    var = jnp.mean(jnp.square(xf - mu), axis=-1, keepdims=True)
    y = (xf - mu) * lax.rsqrt(var + EPS)
    return (y * g.astype(jnp.float32) + b.astype(jnp.float32)).astype(x.dtype)


def causal_dwconv(x, w, b):
    k, c = w.shape
    y = lax.conv_general_dilated(
        x, w[:, None, :].astype(x.dtype), window_strides=(1,), padding=((k - 1, 0),),
        dimension_numbers=("NWC", "WIO", "NWC"), feature_group_count=c)
    return y + b.astype(x.dtype)


def rope(t, positions):
    d = t.shape[-1]
    inv_freq = ROPE_THETA ** (-jnp.arange(0, d, 2, dtype=jnp.float32) / d)
    ang = positions.astype(jnp.float32)[:, :, None] * inv_freq
    cos = jnp.cos(ang)[:, :, None, None, :]
    sin = jnp.sin(ang)[:, :, None, None, :]
    tf = t.astype(jnp.float32)
    t1, t2 = jnp.split(tf, 2, axis=-1)
    out = jnp.concatenate([t1 * cos - t2 * sin, t2 * cos + t1 * sin], axis=-1)
    return out.astype(t.dtype)


def diff_attention(q, k, v, lam, positions):
    bsz, s, h, _, d = q.shape
    e = v.shape[-1]
    nblk = s // Q_BLOCK
    scale = d ** -0.5
    qb = q.reshape(bsz, nblk, Q_BLOCK, h, 2, d).transpose(1, 0, 2, 3, 4, 5)
    pb = positions.reshape(bsz, nblk, Q_BLOCK).transpose(1, 0, 2)
    neg = jnp.finfo(jnp.float32).min

    def one_block(args):
        qi, pi = args
        sc = jnp.einsum("bqhcd,bkhcd->bhcqk", qi, k,
                        preferred_element_type=jnp.float32) * scale
        mask = pi[:, :, None] >= positions[:, None, :]
        sc = jnp.where(mask[:, None, None], sc, neg)
        p = jax.nn.softmax(sc, axis=-1)
        a = p[:, :, 0] - lam * p[:, :, 1]
        return jnp.einsum("bhqk,bkhe->bqhe", a.astype(v.dtype), v)

    out = lax.map(one_block, (qb, pb))
    return out.transpose(1, 0, 2, 3, 4).reshape(bsz, s, h, e)


def setup_inputs(seed: int = 0) -> dict:
    key = jax.random.key(seed)
    ks = jax.random.split(key, 24)
    f32 = jnp.float32
    L = DEPTH

    def nrm(k, shape, scale):
        return jax.random.normal(k, shape, f32) * scale

    def gain(k, shape):
        return 1.0 + 0.05 * jax.random.normal(k, shape, f32)

    x = jax.random.normal(ks[0], (BATCH, SEQ, D_MODEL), f32)
    positions = jnp.broadcast_to(jnp.arange(SEQ, dtype=jnp.int32)[None, :], (BATCH, SEQ))
    return {
        "x": x,
        "positions": positions,
        "attn_pre_g": gain(ks[1], (L, D_MODEL)),
        "attn_post_g": gain(ks[2], (L, D_MODEL)),
        "w_in": nrm(ks[3], (L, D_MODEL, IN_COLS), D_MODEL ** -0.5),
        "lambda_q1": nrm(ks[4], (L, DA_HEAD_DIM), 0.1),
        "lambda_k1": nrm(ks[5], (L, DA_HEAD_DIM), 0.1),
        "lambda_q2": nrm(ks[6], (L, DA_HEAD_DIM), 0.1),
        "lambda_k2": nrm(ks[7], (L, DA_HEAD_DIM), 0.1),
        "subln_g": gain(ks[8], (L, DA_V_DIM)),
        "conv_dw_w": nrm(ks[9], (L, CONV_K, CONV_WIDTH), CONV_K ** -0.5),
        "conv_dw_b": nrm(ks[10], (L, CONV_WIDTH), 0.02),
        "conv_ln_g": gain(ks[11], (L, CONV_WIDTH)),
        "conv_ln_b": nrm(ks[12], (L, CONV_WIDTH), 0.02),
        "w_out": nrm(ks[13], (L, ATTN_WIDTH + CONV_WIDTH, D_MODEL), (ATTN_WIDTH + CONV_WIDTH) ** -0.5),
        "ffn_pre_g": gain(ks[14], (L, D_MODEL)),
        "ffn_post_g": gain(ks[15], (L, D_MODEL)),
        "w_up": nrm(ks[16], (L, D_MODEL, 2 * D_FF), D_MODEL ** -0.5),
        "ffn_dw_w": nrm(ks[17], (L, FFN_CONV_K, 2 * D_FF), FFN_CONV_K ** -0.5),
        "ffn_dw_b": nrm(ks[18], (L, 2 * D_FF), 0.02),
        "w_down": nrm(ks[19], (L, D_FF, D_MODEL), D_FF ** -0.5),
    }


def reference(x, positions, attn_pre_g, attn_post_g, w_in, lambda_q1, lambda_k1,
              lambda_q2, lambda_k2, subln_g, conv_dw_w, conv_dw_b, conv_ln_g,
              conv_ln_b, w_out, ffn_pre_g, ffn_post_g, w_up, ffn_dw_w, ffn_dw_b,
              w_down):
    bsz, s, _ = x.shape
    for l in range(DEPTH):
        h = rms_norm(x, attn_pre_g[l])
        proj = h @ w_in[l]
        q, k, v, cg = jnp.split(
            proj, [ATTN_WIDTH, 2 * ATTN_WIDTH, 3 * ATTN_WIDTH], axis=-1)

        q = rope(q.reshape(bsz, s, DA_HEADS, 2, DA_HEAD_DIM), positions)
        k = rope(k.reshape(bsz, s, DA_HEADS, 2, DA_HEAD_DIM), positions)
        v = v.reshape(bsz, s, DA_HEADS, DA_V_DIM)
        lambda_init = 0.8 - 0.6 * math.exp(-0.3 * l)
        lam = (jnp.exp(jnp.sum(lambda_q1[l].astype(jnp.float32) * lambda_k1[l].astype(jnp.float32)))
               - jnp.exp(jnp.sum(lambda_q2[l].astype(jnp.float32) * lambda_k2[l].astype(jnp.float32)))
               + lambda_init)
        att = diff_attention(q, k, v, lam, positions)
        att = rms_norm(att, subln_g[l]) * (1.0 - lambda_init)
        att = att.reshape(bsz, s, ATTN_WIDTH)

        ga, gb = jnp.split(cg, 2, axis=-1)
        c = ga * jax.nn.sigmoid(gb)
        c = causal_dwconv(c, conv_dw_w[l], conv_dw_b[l])
        c = jax.nn.silu(layer_norm(c, conv_ln_g[l], conv_ln_b[l]))

        mix = jnp.concatenate([att, c], axis=-1) @ w_out[l]
        x = x + rms_norm(mix, attn_post_g[l])

        h = rms_norm(x, ffn_pre_g[l])
        u = causal_dwconv(h @ w_up[l], ffn_dw_w[l], ffn_dw_b[l])
        gate, up = jnp.split(u, 2, axis=-1)
        y = (jax.nn.gelu(gate, approximate=True) * up) @ w_down[l]
        x = x + rms_norm(y, ffn_post_g[l])
    return x
```

```python
import math
from contextlib import ExitStack

import numpy as np
import ml_dtypes

import concourse.bass as bass
import concourse.mybir as mybir
from concourse.bass_utils import run_bass_kernel_spmd

F32 = mybir.dt.float32
BF16 = mybir.dt.bfloat16
I32 = mybir.dt.int32
ALU = mybir.AluOpType
AF = mybir.ActivationFunctionType

NCORES = 8
S = 16384
D = 1024
BLK = 512
NOWN = 4
TOWN = NOWN * BLK
HALO = 32
XW = HALO + BLK
NH = 4
DFF = 2816
NPAIR = DFF // 128
INX = 3584
EPS = 1e-6
LAMBDA_INIT = 0.8 - 0.6 * math.exp(-0.3 * 0)
PI = math.pi

V_G1 = 0
V_G2 = 8
V_CW = 16
V_CB = 140
V_LG = 144
V_LB = 148
V_FW = 152
V_FB = 284
V_IF = 328
V_SG = 329
V_NPS = 330
NV = 332
R_GP1 = 0
R_GP2 = 1024
R_SUB = 2048
R_LQ1 = 2176
R_LK1 = 2240
R_LQ2 = 2304
R_LK2 = 2368
NR = 2432

SAME_ENGINE_SYNC = True
_STEP_LIMIT = 1000
_STOP_AFTER = 0


class Buf:
    __slots__ = ("name", "last_w", "readers")

    def __init__(self, name):
        self.name = name
        self.last_w = None
        self.readers = []


class Sched:
    ENG = ("pe", "act", "dve", "pool", "sp")

    def __init__(self, nc, stack):
        self.nc = nc
        self.stack = stack
        self.sems = {e: stack.enter_context(nc.semaphore("s_" + e)) for e in self.ENG}
        self.cnt = {e: 0 for e in self.ENG}
        self.prog = {e: [] for e in self.ENG}
        self.waited = {e: {} for e in self.ENG}
        self.dcnt = {}
        self.nbuf = 0
        self.disabled = False

    def buf(self, name=None):
        self.nbuf += 1
        return Buf(name or ("b%d" % self.nbuf))

    def _need(self, eng, toks):
        best = {}
        for t in toks:
            if t is None:
                continue
            k, v = t
            if not SAME_ENGINE_SYNC and k == eng:
                continue
            if v > best.get(k, 0):
                best[k] = v
        w = self.waited[eng]
        for k, v in best.items():
            if w.get(k, 0) >= v:
                continue
            w[k] = v
            self.prog[eng].append(("w", k, v))

    @staticmethod
    def _deps(reads, writes, after):
        toks = [b.last_w for b in reads] + [b.last_w for b in writes]
        for b in writes:
            toks.extend(b.readers)
        toks.extend(after)
        return toks

    def op(self, eng, fn, reads=(), writes=(), sets=(), after=()):
        if self.disabled:
            return None
        self._need(eng, self._deps(reads, writes, after))
        self.cnt[eng] += 1
        tok = (eng, self.cnt[eng])
        self.prog[eng].append(("o", fn))
        for b in reads:
            b.readers.append(tok)
        for b in writes:
            b.last_w = tok
            b.readers = []
        for b in sets:
            b.last_w = tok
        return tok

    def dma(self, q, fn, owner, reads=(), writes=(), after=(), n=1):
        if self.disabled:
            return None
        self._need(q, self._deps(reads, writes, after))
        key = "d:" + owner.name
        if key not in self.sems:
            self.sems[key] = self.stack.enter_context(self.nc.semaphore("d_" + owner.name))
            self.dcnt[key] = 0
        self.dcnt[key] += 16 * n
        tok = (key, self.dcnt[key])
        self.prog[q].append(("d", fn, key))
        for b in reads:
            b.readers.append(tok)
        for b in writes:
            b.last_w = tok
            b.readers = []
        return tok

    def barrier(self):
        if self.disabled:
            return
        toks = [(e, self.cnt[e]) for e in self.ENG if self.cnt[e] > 0]
        toks += [(k, v) for k, v in self.dcnt.items()]
        for e in self.ENG:
            self._need(e, toks)

    def wait_all(self, eng, toks):
        if self.disabled:
            return
        self._need(eng, toks)

    def replay(self, eng, e):
        for it in self.prog[eng]:
            if it[0] == "w":
                e.wait_ge(self.sems[it[1]], it[2])
            elif it[0] == "o":
                ins = it[1](e)
                ins.then_inc(self.sems[eng], 1)
            else:
                it[1](e, self.sems[it[2]])


class Arena:
    def __init__(self, t32, nwords):
        self.t32 = t32
        self.tbf = t32[:, :].bitcast(BF16)
        self.nwords = nwords
        self.off = 0

    def reset(self, off=0):
        self.off = off

    def f32(self, n):
        o = self.off
        self.off += n
        assert self.off <= self.nwords, ("arena overflow", self.off, self.nwords)
        return self.t32[:, o:o + n]

    def bf(self, n):
        n2 = (n + 1) // 2
        o = self.off
        self.off += n2
        assert self.off <= self.nwords, ("arena overflow", self.off, self.nwords)
        return self.tbf[:, 2 * o:2 * o + n]


def build(debug=0):
    nc = bass.Bass("TRN2", target_bir_lowering=False)
    xT_d = nc.dram_tensor("xT", [D, NOWN * XW], F32, kind="ExternalInput")
    xtok_d = nc.dram_tensor("xtok", [TOWN, D], F32, kind="ExternalInput")
    pos_d = nc.dram_tensor("pos", [1, TOWN], I32, kind="ExternalInput")
    win_d = nc.dram_tensor("win", [D, INX], F32, kind="ExternalInput")
    wout_d = nc.dram_tensor("wout", [D, D], F32, kind="ExternalInput")
    wup_d = nc.dram_tensor("wup", [D, 2 * DFF], F32, kind="ExternalInput")
    wdn_d = nc.dram_tensor("wdn", [DFF, D], F32, kind="ExternalInput")
    vecs_d = nc.dram_tensor("vecs", [128, NV], F32, kind="ExternalInput")
    rows_d = nc.dram_tensor("rows", [1, NR], F32, kind="ExternalInput")
    cbf_d = nc.dram_tensor("cbf", [128, 256], BF16, kind="ExternalInput")
    mask_d = nc.dram_tensor("masks", [NOWN * 8 * 128, 4 * BLK], BF16, kind="ExternalInput")
    sel_d = nc.dram_tensor("sel", [64, 8], BF16, kind="ExternalInput")
    out_d = nc.dram_tensor("out", [TOWN, D], F32, kind="ExternalOutput")
    kvloc = nc.dram_tensor("kvloc", [4096, BLK], BF16)
    kvall = nc.dram_tensor("kvall", [NCORES * 4096, BLK], BF16)
    halloc = nc.dram_tensor("halloc", [8, D], BF16)
    halall = nc.dram_tensor("halall", [64, D], BF16)
    x1_d = nc.dram_tensor("x1s", [TOWN, D], F32)
    dbg = {}
    if debug:
        dbg["qt"] = nc.dram_tensor("dbg_qt", [128, NH * TOWN], BF16, kind="ExternalOutput")
        dbg["ct"] = nc.dram_tensor("dbg_ct", [128, 4 * TOWN], BF16, kind="ExternalOutput")
        dbg["kv"] = nc.dram_tensor("dbg_kv", [4096, BLK], BF16, kind="ExternalOutput")
        dbg["att"] = nc.dram_tensor("dbg_att", [128, 16 * 512], BF16, kind="ExternalOutput")
        dbg["x1"] = nc.dram_tensor("dbg_x1", [TOWN, D], F32, kind="ExternalOutput")

    AW = 49600
    with ExitStack() as stack:
        arena_t = stack.enter_context(nc.sbuf_tensor("arena", [128, AW], F32))
        vecs = stack.enter_context(nc.sbuf_tensor("vecs_sb", [128, NV], F32))
        rows = stack.enter_context(nc.sbuf_tensor("rows_sb", [128, NR], F32))
        cbf = stack.enter_context(nc.sbuf_tensor("cbf_sb", [128, 256], BF16))
        onesf = stack.enter_context(nc.sbuf_tensor("onesf", [128, 128], F32))
        smalls = stack.enter_context(nc.sbuf_tensor("smalls", [128, 80], F32))
        subg8 = stack.enter_context(nc.sbuf_tensor("subg8", [128, 128], F32))
        psum = stack.enter_context(nc.psum_tensor("ps", [128, 8 * 512], F32))
        block = stack.enter_context(nc.Block())
        sc = Sched(nc, stack)
        A = Arena(arena_t, AW)
        ident = cbf[:, 0:128]
        psbf = psum[:, :].bitcast(BF16)

        def bank(k, n=512, o=0):
            return psum[:, k * 512 + o:k * 512 + o + n]

        pbuf = [sc.buf("psb%d" % k) for k in range(8)]
        final_toks = []


        def early_finish():
            final_toks.append(sc.dma("sp", lambda e, s: e.dma_start(out=out_d[:, :], in_=xtok_d[:, :]).then_inc(s, 16), sc.buf("early")))
            sc.wait_all("sp", final_toks)
            sc.disabled = True
        b_vecs, b_rows, b_cbf, b_ones, b_small, b_subg = (sc.buf("vecs"), sc.buf("rows"), sc.buf("cbf"),
                                                          sc.buf("ones"), sc.buf("smalls"), sc.buf("subg"))
        sc.dma("sp", lambda e, s: e.dma_start(out=vecs[:, :], in_=vecs_d[:, :]).then_inc(s, 16), b_vecs, writes=[b_vecs])
        sc.dma("sp", lambda e, s: e.dma_start(out=rows[:, :], in_=rows_d[0:1, :].partition_broadcast(128)).then_inc(s, 16),
               b_rows, writes=[b_rows])
        sc.dma("sp", lambda e, s: e.dma_start(out=cbf[:, :], in_=cbf_d[:, :]).then_inc(s, 16), b_cbf, writes=[b_cbf])
        sc.op("dve", lambda e: e.memset(onesf[:, :], 1.0), writes=[b_ones])
        epsb = smalls[:, 6:7]
        b_eps = sc.buf("eps")
        sc.op("dve", lambda e: e.memset(epsb, EPS), writes=[b_eps])
        sc.op("dve", lambda e: e.tensor_tensor(out=smalls[:, 8:8 + 64], in0=rows[:, R_LQ1:R_LQ1 + 64],
                                               in1=rows[:, R_LK1:R_LK1 + 64], op=ALU.mult), reads=[b_rows], writes=[b_small])
        sc.op("dve", lambda e: e.reduce_sum(out=smalls[:, 2:3], in_=smalls[:, 8:8 + 64], axis=mybir.AxisListType.X),
              reads=[b_small], writes=[b_small])
        sc.op("dve", lambda e: e.tensor_tensor(out=smalls[:, 8:8 + 64], in0=rows[:, R_LQ2:R_LQ2 + 64],
                                               in1=rows[:, R_LK2:R_LK2 + 64], op=ALU.mult), reads=[b_rows, b_small], writes=[b_small])
        sc.op("dve", lambda e: e.reduce_sum(out=smalls[:, 3:4], in_=smalls[:, 8:8 + 64], axis=mybir.AxisListType.X),
              reads=[b_small], writes=[b_small])
        sc.op("act", lambda e: e.activation(out=smalls[:, 4:6], in_=smalls[:, 2:4], func=AF.Exp), reads=[b_small], writes=[b_small])
        sc.op("dve", lambda e: e.tensor_tensor(out=smalls[:, 0:1], in0=smalls[:, 4:5], in1=smalls[:, 5:6], op=ALU.subtract),
              reads=[b_small], writes=[b_small])
        sc.op("dve", lambda e: e.tensor_scalar(out=smalls[:, 1:2], in0=smalls[:, 0:1], scalar1=-1.0, scalar2=-LAMBDA_INIT,
                                               op0=ALU.mult, op1=ALU.add), reads=[b_small], writes=[b_small])
        nlam = smalls[:, 1:2]
        sc.op("dve", lambda e: e.tensor_scalar(out=subg8[:, :], in0=rows[:, R_SUB:R_SUB + 128], scalar1=1.0 - LAMBDA_INIT,
                                               scalar2=None, op0=ALU.mult), reads=[b_rows], writes=[b_subg])

        A.reset(0)
        QT = A.bf(NH * TOWN).rearrange("p (h t) -> p h t", h=NH)
        CT = A.bf(4 * TOWN).rearrange("p (m t) -> p m t", m=4)
        ATT = A.bf(16 * 512).rearrange("p (t e) -> p t e", t=16)
        R1_END = A.off
        b_QT = [[sc.buf() for _ in range(NOWN)] for _ in range(NH)]
        b_CT = [sc.buf() for _ in range(NOWN)]
        b_ATT = [sc.buf() for _ in range(16)]

        winb = A.bf(8 * INX).rearrange("p (k n) -> p k n", k=8)
        b_win = [sc.buf("win%d" % k) for k in range(8)]
        win_v = win_d.ap().rearrange("(k p) n -> p k n", p=128)
        for kc in range(8):
            sc.dma("pool", (lambda e, s, kc=kc: e.dma_start(out=winb[:, kc, :], in_=win_v[:, kc, :]).then_inc(s, 16)),
                   b_win[kc], writes=[b_win[kc]])
        off_x = A.off
        xTs = A.f32(8 * XW).rearrange("p (k n) -> p k n", k=8)
        sqs = [A.f32(XW) for _ in range(2)]
        rstd = A.f32(XW)
        end1 = A.off
        A.reset(off_x)
        csb = A.f32(4 * XW).rearrange("p (m n) -> p m n", m=4)
        acc = A.f32(4 * BLK).rearrange("p (m n) -> p m n", m=4)
        mean_sb = A.f32(BLK)
        lrstd = A.f32(BLK)
        sig = A.f32(BLK)
        sigh = A.f32(4 * HALO)
        A.reset(max(end1, A.off))
        hT_all = [A.bf(8 * XW).rearrange("p (k n) -> p k n", k=8) for _ in range(NOWN)]
        posi = A.f32(BLK).bitcast(I32)
        ang = A.f32(BLK)
        ang2 = A.f32(BLK)
        rk_f = A.f32(BLK)
        rk_i = A.f32(BLK).bitcast(I32)
        C1 = 6.28125
        C2 = 2 * PI - 6.28125
        COS = A.f32(BLK)
        SIN = A.f32(BLK)
        tmpA = [A.f32(BLK) for _ in range(4)]
        ktst = A.bf(NH * BLK).rearrange("p (h t) -> p h t", h=NH)
        vst = A.bf(4 * 512).rearrange("p (k n) -> p k n", k=4)
        ptmp = A.f32(BLK)
        b_ptmp = sc.buf("ptmp")
        b_xT, b_sq, b_hT_all, b_rstd = sc.buf("xT"), [sc.buf(), sc.buf()], [sc.buf("hT%d" % q) for q in range(NOWN)], sc.buf("rstd")
        b_pos, b_ang, b_ang2, b_cos, b_sin = sc.buf("posi"), sc.buf(), sc.buf(), sc.buf(), sc.buf()
        b_rk, b_rki = sc.buf(), sc.buf()
        b_tmpA = [sc.buf() for _ in range(4)]
        b_ktst, b_vst, b_csb = sc.buf("ktst"), sc.buf("vst"), [sc.buf() for _ in range(4)]
        b_csbh = sc.buf("csbh")
        b_acc = [sc.buf() for _ in range(4)]
        b_ysq, b_mean, b_lrstd, b_sig, b_sigh = sc.buf(), sc.buf(), sc.buf(), sc.buf(), sc.buf()
        xT_v = xT_d.ap().rearrange("(k p) n -> p k n", p=128)
        kvloc_v = kvloc.ap()
        kv_toks = []
        rr = [0]

        def nb():
            k = rr[0] % 8
            rr[0] += 1
            return k

        def mm_group(bk, lhs_list, rhs_list, n, o=0, reads=()):
            def fn(e, bk=bk, lhs_list=lhs_list, rhs_list=rhs_list, n=n, o=o):
                ins = None
                L = len(lhs_list)
                for i in range(L):
                    ins = e.matmul(bank(bk, n, o), lhsT=lhs_list[i], rhs=rhs_list[i], start=(i == 0), stop=(i == L - 1))
                return ins
            return sc.op("pe", fn, reads=list(reads), writes=[pbuf[bk]])

        for j in range(NOWN):
            c0 = j * XW
            hT = hT_all[j]
            b_hT = b_hT_all[j]
            sc.dma("sp", (lambda e, s, c0=c0: e.dma_start(out=xTs[:, :, :], in_=xT_v[:, :, c0:c0 + XW]).then_inc(s, 16)),
                   b_xT, writes=[b_xT])
            bk_a, bk_b = nb(), nb()
            for kc in range(8):
                sq = sqs[kc % 2]
                sc.op("act", (lambda e, sq=sq, kc=kc: e.activation(out=sq, in_=xTs[:, kc, :], func=AF.Square)),
                      reads=[b_xT], writes=[b_sq[kc % 2]])

                def fn(e, sq=sq, kc=kc, bk_a=bk_a, bk_b=bk_b):
                    e.matmul(bank(bk_a), lhsT=onesf[:, :], rhs=sq[:, HALO:XW], start=(kc == 0), stop=(kc == 7))
                    return e.matmul(bank(bk_b, HALO), lhsT=onesf[:, :], rhs=sq[:, 0:HALO], start=(kc == 0), stop=(kc == 7))
                sc.op("pe", fn, reads=[b_sq[kc % 2], b_ones],
                      writes=([pbuf[bk_a], pbuf[bk_b]] if kc == 0 else []), sets=([pbuf[bk_a], pbuf[bk_b]] if kc == 7 else []))
            sc.op("act", (lambda e, bk_a=bk_a: e.activation(out=rstd[:, HALO:XW], in_=bank(bk_a), func=AF.Sqrt, scale=1.0 / D, bias=epsb)),
                  reads=[pbuf[bk_a], b_eps], writes=[b_rstd])
            sc.op("act", (lambda e, bk_b=bk_b: e.activation(out=rstd[:, 0:HALO], in_=bank(bk_b, HALO), func=AF.Sqrt, scale=1.0 / D, bias=epsb)),
                  reads=[pbuf[bk_b], b_eps], writes=[b_rstd])
            sc.op("dve", lambda e: e.reciprocal(out=rstd[:, :], in_=rstd[:, :]), reads=[b_rstd], writes=[b_rstd])
            for kc in range(8):
                sc.op("dve", (lambda e, kc=kc, hT=hT: e.scalar_tensor_tensor(out=hT[:, kc, :], in0=xTs[:, kc, :],
                                                                      scalar=vecs[:, V_G1 + kc:V_G1 + kc + 1], in1=rstd[:, :],
                                                                      op0=ALU.mult, op1=ALU.mult)),
                      reads=[b_xT, b_rstd, b_vecs], writes=([b_hT] if kc == 0 else []), sets=([b_hT] if kc == 7 else []))
            sc.dma("sp", (lambda e, s, j=j: e.dma_start(out=posi, in_=pos_d[0:1, j * BLK:(j + 1) * BLK].partition_broadcast(128)).then_inc(s, 16)),
                   b_pos, writes=[b_pos])
            sc.op("dve", lambda e: e.tensor_copy(out=ang, in_=posi), reads=[b_pos], writes=[b_ang])
            sc.op("dve", lambda e: e.tensor_scalar(out=ang, in0=ang, scalar1=vecs[:, V_IF:V_IF + 1], scalar2=None, op0=ALU.mult),
                  reads=[b_ang, b_vecs], writes=[b_ang])
            sc.op("dve", lambda e: e.tensor_scalar(out=ang2, in0=ang, scalar1=0.5 * PI, scalar2=None, op0=ALU.add),
                  reads=[b_ang], writes=[b_ang2])
            for (av, bv) in ((ang, b_ang), (ang2, b_ang2)):
                sc.op("dve", (lambda e, av=av: e.tensor_scalar(out=rk_f, in0=av, scalar1=1.0 / (2 * PI), scalar2=None, op0=ALU.mult)),
                      reads=[bv], writes=[b_rk])
                sc.op("dve", lambda e: e.tensor_copy(out=rk_i, in_=rk_f), reads=[b_rk], writes=[b_rki])
                sc.op("dve", lambda e: e.tensor_copy(out=rk_f, in_=rk_i), reads=[b_rki], writes=[b_rk])
                sc.op("dve", (lambda e, av=av: e.scalar_tensor_tensor(out=av, in0=rk_f, scalar=-C1, in1=av, op0=ALU.mult, op1=ALU.add)),
                      reads=[b_rk, bv], writes=[bv])
                sc.op("dve", (lambda e, av=av: e.scalar_tensor_tensor(out=av, in0=rk_f, scalar=-C2, in1=av, op0=ALU.mult, op1=ALU.add)),
                      reads=[b_rk, bv], writes=[bv])
                sc.op("dve", (lambda e, av=av: e.tensor_scalar(out=rk_f, in0=av, scalar1=PI, scalar2=-2 * PI, op0=ALU.is_gt, op1=ALU.mult)),
                      reads=[bv, b_rk], writes=[b_rk])
                sc.op("dve", (lambda e, av=av: e.tensor_tensor(out=av, in0=av, in1=rk_f, op=ALU.add)), reads=[bv, b_rk], writes=[bv])
                sc.op("dve", (lambda e, av=av: e.tensor_scalar(out=av, in0=av, scalar1=-PI, scalar2=PI, op0=ALU.max, op1=ALU.min)),
                      reads=[bv], writes=[bv])
            sc.op("act", lambda e: e.activation(out=COS, in_=ang2, func=AF.Sin), reads=[b_ang2], writes=[b_cos])
            sc.op("act", lambda e: e.activation(out=SIN, in_=ang, func=AF.Sin, scale=vecs[:, V_SG:V_SG + 1]),
                  reads=[b_ang, b_vecs], writes=[b_sin])
            for which in range(2):
                for h in range(NH):
                    ca = (0 if which == 0 else 8) + h
                    cb = ca + 4
                    bka, bkb = nb(), nb()
                    mm_group(bka, [winb[:, kc, ca * 128:(ca + 1) * 128] for kc in range(8)], [hT[:, kc, HALO:XW] for kc in range(8)],
                             512, reads=[b_hT] + b_win)
                    mm_group(bkb, [winb[:, kc, cb * 128:(cb + 1) * 128] for kc in range(8)], [hT[:, kc, HALO:XW] for kc in range(8)],
                             512, reads=[b_hT] + b_win)
                    t1, t2 = (0, 1) if (h % 2 == 0) else (2, 3)
                    sc.op("dve", (lambda e, bka=bka, t1=t1: e.tensor_tensor(out=tmpA[t1], in0=bank(bka), in1=COS, op=ALU.mult)),
                          reads=[pbuf[bka], b_cos], writes=[b_tmpA[t1]])
                    sc.op("dve", (lambda e, bkb=bkb, t2=t2: e.tensor_tensor(out=tmpA[t2], in0=bank(bkb), in1=SIN, op=ALU.mult)),
                          reads=[pbuf[bkb], b_sin], writes=[b_tmpA[t2]])
                    if which == 0:
                        dst = QT[:, h, j * BLK:(j + 1) * BLK]
                        sc.op("pool", (lambda e, dst=dst, t1=t1, t2=t2: e.tensor_tensor(out=dst, in0=tmpA[t1], in1=tmpA[t2], op=ALU.add)),
                              reads=[b_tmpA[t1], b_tmpA[t2]], writes=[b_QT[h][j]])
                    else:
                        dst = ktst[:, h, :]
                        sc.op("pool", (lambda e, dst=dst, t1=t1, t2=t2: e.tensor_tensor(out=dst, in0=tmpA[t1], in1=tmpA[t2], op=ALU.add)),
                              reads=[b_tmpA[t1], b_tmpA[t2]], writes=([b_ktst] if h == 0 else []), sets=([b_ktst] if h == NH - 1 else []))
            for h in range(NH):
                r0 = 2048 + (h * 4 + j) * 128
                kv_toks.append(sc.dma("sp", (lambda e, s, h=h, r0=r0: e.dma_start(out=kvloc_v[r0:r0 + 128, :], in_=ktst[:, h, :]).then_inc(s, 16)),
                                      b_ktst, reads=[b_ktst]))
            for tt in range(4):
                bk = nb()
                mm_group(bk, [hT[:, kc, HALO + tt * 128:HALO + (tt + 1) * 128] for kc in range(8)],
                         [winb[:, kc, 16 * 128:20 * 128] for kc in range(8)], 512, reads=[b_hT] + b_win)
                sc.op("act", (lambda e, bk=bk, tt=tt: e.activation(out=vst[:, tt, :], in_=bank(bk), func=AF.Copy)),
                      reads=[pbuf[bk]], writes=([b_vst] if tt == 0 else []), sets=([b_vst] if tt == 3 else []))
            for h in range(NH):
                r0 = (h * 4 + j) * 128
                kv_toks.append(sc.dma("sp", (lambda e, s, h=h, r0=r0: e.dma_start(
                    out=kvloc_v[r0:r0 + 128, :].rearrange("p (k e) -> p k e", k=4),
                    in_=vst[:, :, h * 128:(h + 1) * 128]).then_inc(s, 16)), b_vst, reads=[b_vst]))
        sc.barrier()
        b_kvall = sc.buf("kvall")
        cc_sem = stack.enter_context(nc.semaphore("cc1"))
        sc.sems["cc1"] = cc_sem
        sc.wait_all("pool", kv_toks)

        def fn_ag(e):
            ins = e.collective_compute("AllGather", ALU.bypass, replica_groups=[list(range(NCORES))],
                                       ins=[kvloc.ap()], outs=[kvall.ap()])
            ins.then_inc(cc_sem, 1)
            return ins
        sc.prog["pool"].append(("ct", fn_ag))
        for j in range(NOWN):
            hT = hT_all[j]
            b_hT = b_hT_all[j]
            bkh = nb()
            for m in range(4):
                ca, cb = 20 + m, 24 + m
                def fnh(e, m=m, ca=ca, cb=cb, bkh=bkh, hT=hT):
                    ins = None
                    for (cc, oo) in ((ca, m * 64), (cb, m * 64 + 32)):
                        for kc in range(8):
                            ins = e.matmul(bank(bkh, HALO, oo), lhsT=winb[:, kc, cc * 128:(cc + 1) * 128], rhs=hT[:, kc, 0:HALO],
                                           start=(kc == 0 and m == 0 and cc == ca), stop=(kc == 7), skip_group_check=True)
                    return ins
                sc.op("pe", fnh, reads=[b_hT] + b_win, writes=([pbuf[bkh]] if m == 0 else []), sets=([pbuf[bkh]] if m == 3 else []))
            hv = bank(bkh, 256).rearrange("p (m two n) -> p m two n", m=4, two=2)
            sc.op("act", (lambda e, hv=hv: e.activation(out=sigh.rearrange("p (m n) -> p m n", m=4), in_=hv[:, :, 1, :], func=AF.Sigmoid)),
                  reads=[pbuf[bkh]], writes=[b_sigh])
            sc.op("dve", (lambda e, hv=hv: e.tensor_tensor(out=csb[:, :, 0:HALO], in0=hv[:, :, 0, :],
                                                           in1=sigh.rearrange("p (m n) -> p m n", m=4), op=ALU.mult)),
                  reads=[pbuf[bkh], b_sigh], writes=[b_csbh])
            for m in range(4):
                ca, cb = 20 + m, 24 + m
                bka, bkb = nb(), nb()
                mm_group(bka, [winb[:, kc, ca * 128:(ca + 1) * 128] for kc in range(8)], [hT[:, kc, HALO:XW] for kc in range(8)],
                         512, reads=[b_hT] + b_win)
                mm_group(bkb, [winb[:, kc, cb * 128:(cb + 1) * 128] for kc in range(8)], [hT[:, kc, HALO:XW] for kc in range(8)],
                         512, reads=[b_hT] + b_win)
                sc.op("act", (lambda e, bkb=bkb: e.activation(out=sig, in_=bank(bkb), func=AF.Sigmoid)), reads=[pbuf[bkb]], writes=[b_sig])
                sc.op("dve", (lambda e, bka=bka, m=m: e.tensor_tensor(out=csb[:, m, HALO:XW], in0=bank(bka), in1=sig, op=ALU.mult)),
                      reads=[pbuf[bka], b_sig], writes=[b_csb[m]])
            for k in range(31):
                for m in range(4):
                    wk = vecs[:, V_CW + m * 31 + k:V_CW + m * 31 + k + 1]
                    src = csb[:, m, 2 + k:2 + k + BLK]
                    if m == 3:
                        if k == 0:
                            sc.op("pool", (lambda e, m=m, wk=wk, src=src: e.tensor_scalar(out=acc[:, m, :], in0=src, scalar1=wk,
                                                                                          scalar2=vecs[:, V_CB + m:V_CB + m + 1],
                                                                                          op0=ALU.mult, op1=ALU.add)),
                                  reads=[b_csb[m], b_csbh, b_vecs], writes=[b_acc[m]])
                        else:
                            sc.op("pool", (lambda e, wk=wk, src=src: e.tensor_scalar(out=ptmp, in0=src, scalar1=wk, scalar2=None, op0=ALU.mult)),
                                  reads=[b_csb[m], b_csbh, b_vecs], writes=[b_ptmp])
                            sc.op("pool", (lambda e, m=m: e.tensor_tensor(out=acc[:, m, :], in0=acc[:, m, :], in1=ptmp, op=ALU.add)),
                                  reads=[b_ptmp, b_acc[m]], writes=[b_acc[m]])
                    elif k == 0:
                        sc.op("dve", (lambda e, m=m, wk=wk, src=src: e.tensor_scalar(out=acc[:, m, :], in0=src, scalar1=wk,
                                                                                     scalar2=vecs[:, V_CB + m:V_CB + m + 1],
                                                                                     op0=ALU.mult, op1=ALU.add)),
                              reads=[b_csb[m], b_csbh, b_vecs], writes=[b_acc[m]])
                    else:
                        sc.op("dve", (lambda e, m=m, wk=wk, src=src: e.scalar_tensor_tensor(out=acc[:, m, :], in0=src, scalar=wk,
                                                                                            in1=acc[:, m, :], op0=ALU.mult, op1=ALU.add)),
                              reads=[b_csb[m], b_csbh, b_acc[m]], writes=[b_acc[m]])
            bkm, bke = nb(), nb()
            for m in range(4):
                sc.op("act", (lambda e, m=m: e.activation(out=tmpA[m], in_=acc[:, m, :], func=AF.Square)),
                      reads=[b_acc[m]], writes=[b_tmpA[m]])
            mm_group(bkm, [onesf[:, :]] * 4, [acc[:, m, :] for m in range(4)], 512, reads=b_acc + [b_ones])
            mm_group(bke, [onesf[:, :]] * 4, [tmpA[m] for m in range(4)], 512, reads=b_tmpA + [b_ones])
            sc.op("act", (lambda e, bkm=bkm: e.activation(out=mean_sb, in_=bank(bkm), func=AF.Copy, scale=1.0 / 512)),
                  reads=[pbuf[bkm]], writes=[b_mean])
            sc.op("dve", lambda e: e.tensor_tensor(out=lrstd, in0=mean_sb, in1=mean_sb, op=ALU.mult), reads=[b_mean], writes=[b_lrstd])
            sc.op("dve", (lambda e, bke=bke: e.scalar_tensor_tensor(out=lrstd, in0=bank(bke), scalar=1.0 / 512, in1=lrstd,
                                                                    op0=ALU.mult, op1=ALU.subtract)),
                  reads=[pbuf[bke], b_lrstd], writes=[b_lrstd])
            sc.op("act", lambda e: e.activation(out=lrstd, in_=lrstd, func=AF.Sqrt, bias=epsb), reads=[b_lrstd, b_eps], writes=[b_lrstd])
            sc.op("dve", lambda e: e.reciprocal(out=lrstd, in_=lrstd), reads=[b_lrstd], writes=[b_lrstd])
            for m in range(4):
                sc.op("dve", (lambda e, m=m: e.tensor_tensor(out=acc[:, m, :], in0=acc[:, m, :], in1=mean_sb, op=ALU.subtract)),
                      reads=[b_acc[m], b_mean], writes=[b_acc[m]])
                sc.op("dve", (lambda e, m=m: e.tensor_tensor(out=acc[:, m, :], in0=acc[:, m, :], in1=lrstd, op=ALU.mult)),
                      reads=[b_acc[m], b_lrstd], writes=[b_acc[m]])
                sc.op("act", (lambda e, m=m, j=j: e.activation(out=CT[:, m, j * BLK:(j + 1) * BLK], in_=acc[:, m, :], func=AF.Silu,
                                                              scale=vecs[:, V_LG + m:V_LG + m + 1], bias=vecs[:, V_LB + m:V_LB + m + 1])),
                      reads=[b_acc[m], b_vecs], writes=([b_CT[j]] if m == 0 else []), sets=([b_CT[j]] if m == 3 else []))

        sc.prog["pool"].append(("cw", "cc1"))
        ag_tok = sc.op("pool", lambda e: e.memset(smalls[:, 7:8], 0.0), writes=[sc.buf("agflag1")])
        if debug & 1:
            final_toks.append(sc.dma("sp", lambda e, s: e.dma_start(out=dbg["kv"][:, :], in_=kvloc_v[:, :]).then_inc(s, 16),
                                     sc.buf("dbgkv"), after=kv_toks))
            bq = sc.buf("dbgqt")
            final_toks.append(sc.dma("sp", lambda e, s: e.dma_start(out=dbg["qt"][:, :], in_=QT.rearrange("p h t -> p (h t)")).then_inc(s, 16),
                                     bq, reads=[b for hh in b_QT for b in hh]))
            final_toks.append(sc.dma("sp", lambda e, s: e.dma_start(out=dbg["ct"][:, :], in_=CT.rearrange("p m t -> p (m t)")).then_inc(s, 16),
                                     bq, reads=b_CT))
        sc.barrier()

        if _STOP_AFTER == 1:
            early_finish()
        A.reset(R1_END)
        NSLOT = 4
        kslot = [A.bf(BLK) for _ in range(NSLOT)]
        vslot = [A.bf(4 * 130).rearrange("p (k n) -> p k n", k=4) for _ in range(NSLOT)]
        b_slot = [sc.buf("kvs%d" % i) for i in range(NSLOT)]
        NP = 4
        Pb = [A.bf(2 * BLK).rearrange("p (c n) -> p c n", c=2) for _ in range(NP)]
        b_P = [sc.buf("P%d" % i) for i in range(NP)]
        msk = A.bf(8 * 4 * BLK).rearrange("p (s k n) -> p s k n", s=8, k=4)
        b_msk = sc.buf("msk")
        Osb = A.f32(4 * 512).rearrange("p (q n) -> p q n", q=4)
        b_Osb = [sc.buf() for _ in range(4)]
        fa = [A.f32(128) for _ in range(2)]
        fd = [A.f32(128) for _ in range(2)]
        fj = A.f32(128)
        dbuf = A.f32(4 * 128).rearrange("p (q n) -> p q n", q=4)
        fs4 = A.f32(32)
        b_dbuf = [sc.buf() for _ in range(4)]
        b_fs4 = sc.buf("fs4")
        fs = A.f32(16).rearrange("p (a b) -> p a b", a=2)
        b_fa, b_fd, b_fs = [sc.buf(), sc.buf()], [sc.buf(), sc.buf()], [sc.buf(), sc.buf()]
        b_fj = sc.buf()
        for i in range(NSLOT):
            sc.op("dve", (lambda e, i=i: e.memset(vslot[i][:, :, 128:130], 1.0)), writes=[b_slot[i]])
        SB = [(0, 1), (2, 3)]
        b_S = [sc.buf("S0"), sc.buf("S1")]
        OB = [4, 5, 6, 7]
        b_O = sc.buf("O")
        mask_v = mask_d.ap().rearrange("(j s p) (k n) -> j p s k n", j=NOWN, s=8, k=4)
        kvall_v = kvall.ap()
        load_ctr = [0]
        s_ctr = [0]
        p_ctr = [0]

        for j in range(NOWN):
            sc.dma("sp", (lambda e, s, j=j: [e.dma_start(out=msk[:, sl, :, :], in_=mask_v[j, :, sl, :, :]).then_inc(s, 16) for sl in range(8)]),
                   b_msk, writes=[b_msk], n=8)
            nsteps = min(8 * j + 8, _STEP_LIMIT)
            for h in range(NH):
                items = []
                for st in range(nsteps):
                    for kt in range(4):
                        items.append((st, kt))
                slot_of = {}

                def issue_load(st, h=h):
                    sl = load_ctr[0] % NSLOT
                    load_ctr[0] += 1
                    slot_of[st] = sl
                    r, jl = st % 8, st // 8
                    rv = r * 4096 + (h * 4 + jl) * 128
                    rk = rv + 2048

                    def fn(e, s, sl=sl, rv=rv, rk=rk):
                        e.dma_start(out=kslot[sl], in_=kvall_v[rk:rk + 128, :]).then_inc(s, 16)
                        e.dma_start(out=vslot[sl][:, :, 0:128], in_=kvall_v[rv:rv + 128, :].rearrange("p (k e) -> p k e", k=4)).then_inc(s, 16)
                    sc.dma("sp", fn, b_slot[sl], writes=[b_slot[sl]], after=[ag_tok], n=2)

                def issue_S(st, kt, h=h, j=j):
                    sb = s_ctr[0] % 2
                    s_ctr[0] += 1
                    sl = slot_of[st]
                    b0, b1 = SB[sb]

                    def fn(e, sl=sl, kt=kt, b0=b0, b1=b1):
                        e.matmul(bank(b0), lhsT=kslot[sl][0:64, kt * 128:(kt + 1) * 128], rhs=QT[0:64, h, j * BLK:(j + 1) * BLK],
                                 start=True, stop=True)
                        return e.matmul(bank(b1), lhsT=kslot[sl][64:128, kt * 128:(kt + 1) * 128], rhs=QT[64:128, h, j * BLK:(j + 1) * BLK],
                                        start=True, stop=True)
                    sc.op("pe", fn, reads=[b_slot[sl], b_QT[h][j]], writes=[b_S[sb]])
                    return sb

                def issue_exp(st, kt, sb):
                    pb = p_ctr[0] % NP
                    p_ctr[0] += 1
                    b0 = SB[sb][0]
                    src = psum[:, b0 * 512:b0 * 512 + 1024].rearrange("p (c n) -> p c n", c=2)
                    sc.op("act", (lambda e, pb=pb, src=src: e.activation(out=Pb[pb][:, :, :], in_=src, func=AF.Exp, scale=0.125)),
                          reads=[b_S[sb]], writes=[b_P[pb]])
                    if st >= 8 * j:
                        sl8 = st - 8 * j
                        for c in range(2):
                            sc.op("dve", (lambda e, pb=pb, c=c, sl8=sl8, kt=kt: e.tensor_tensor(out=Pb[pb][:, c, :], in0=Pb[pb][:, c, :],
                                                                                              in1=msk[:, sl8, kt, :], op=ALU.mult)),
                                  reads=[b_P[pb], b_msk], writes=[b_P[pb]])
                    return pb

                def issue_PV(st, kt, pb, first, last):
                    sl = slot_of[st]

                    def fn(e, sl=sl, kt=kt, pb=pb, first=first):
                        ins = None
                        for qc in range(4):
                            for c in range(2):
                                ins = e.matmul(bank(OB[qc], 129, c * 256), lhsT=Pb[pb][:, c, qc * 128:(qc + 1) * 128],
                                               rhs=vslot[sl][:, kt, 0:129], start=(first and c == 0), stop=last,
                                               skip_group_check=True)
                        return ins
                    sc.op("pe", fn, reads=[b_P[pb], b_slot[sl]], writes=([b_O] if first else []), sets=([b_O] if last else []))

                issue_load(0)
                if nsteps > 1:
                    issue_load(1)
                pend = None
                n_it = len(items)
                sb_next = issue_S(*items[0])
                pend = []
                for idx in range(n_it):
                    st, kt = items[idx]
                    pb = issue_exp(st, kt, sb_next)
                    pend.append((st, kt, pb, idx))
                    if idx + 1 < n_it:
                        st2, kt2 = items[idx + 1]
                        if kt2 == 0 and st2 + 1 < nsteps:
                            issue_load(st2 + 1)
                        sb_next = issue_S(st2, kt2)
                    if len(pend) > 1:
                        pst, pkt, ppb, pidx = pend.pop(0)
                        issue_PV(pst, pkt, ppb, first=(pidx == 0), last=False)
                while pend:
                    pst, pkt, ppb, pidx = pend.pop(0)
                    issue_PV(pst, pkt, ppb, first=(pidx == 0), last=(len(pend) == 0))
                for qc in range(4):
                    sc.op("dve", (lambda e, qc=qc: e.tensor_copy(out=Osb[:, qc, 0:385], in_=bank(OB[qc], 385))),
                          reads=[b_O], writes=[b_Osb[qc]])
                for qc in range(4):
                    a, ba = fa[qc % 2], b_fa[qc % 2]
                    sc.op("dve", (lambda e, qc=qc: e.reciprocal(out=fs4[:, qc:qc + 1], in_=Osb[:, qc, 128:129])), reads=[b_Osb[qc]], writes=[b_fs4])
                    sc.op("dve", (lambda e, qc=qc: e.reciprocal(out=fs4[:, 4 + qc:5 + qc], in_=Osb[:, qc, 384:385])), reads=[b_Osb[qc], b_fs4], writes=[b_fs4])
                    sc.op("dve", (lambda e, qc=qc: e.tensor_tensor(out=fs4[:, 8 + qc:9 + qc], in0=fs4[:, 4 + qc:5 + qc], in1=nlam, op=ALU.mult)),
                          reads=[b_fs4, b_small], writes=[b_fs4])
                    sc.op("dve", (lambda e, qc=qc, a=a: e.tensor_scalar(out=a, in0=Osb[:, qc, 0:128], scalar1=fs4[:, qc:qc + 1], scalar2=None,
                                                                         op0=ALU.mult)), reads=[b_Osb[qc], b_fs4], writes=[ba])
                    sc.op("dve", (lambda e, qc=qc, a=a: e.scalar_tensor_tensor(out=dbuf[:, qc, :], in0=Osb[:, qc, 256:384], scalar=fs4[:, 8 + qc:9 + qc], in1=a,
                                                                                op0=ALU.mult, op1=ALU.add)),
                          reads=[b_Osb[qc], b_fs4, ba], writes=[b_dbuf[qc]])
                    sc.op("dve", (lambda e, qc=qc: e.scalar_tensor_tensor(out=fj, in0=dbuf[:, qc, :], scalar=1.0, in1=dbuf[:, qc, :], op0=ALU.mult, op1=ALU.mult,
                                                                           accum_out=fs4[:, 12 + qc:13 + qc])), reads=[b_dbuf[qc], b_fs4], writes=[b_fj, b_fs4])
                sc.op("act", lambda e: e.activation(out=fs4[:, 16:20], in_=fs4[:, 12:16], func=AF.Sqrt, scale=1.0 / 128, bias=epsb),
                      reads=[b_fs4, b_eps], writes=[b_fs4])
                sc.op("dve", lambda e: e.reciprocal(out=fs4[:, 16:20], in_=fs4[:, 16:20]), reads=[b_fs4], writes=[b_fs4])
                for qc in range(4):
                    tile_i = j * 4 + qc
                    sc.op("dve", (lambda e, qc=qc, tile_i=tile_i, h=h: e.scalar_tensor_tensor(
                        out=ATT[:, tile_i, h * 128:(h + 1) * 128], in0=dbuf[:, qc, :], scalar=fs4[:, 16 + qc:17 + qc], in1=subg8[:, :], op0=ALU.mult, op1=ALU.mult)),
                        reads=[b_dbuf[qc], b_fs4, b_subg], writes=[b_ATT[tile_i]])
        if debug & 2:
            final_toks.append(sc.dma("sp", lambda e, s: e.dma_start(out=dbg["att"][:, :], in_=ATT.rearrange("p t e -> p (t e)")).then_inc(s, 16),
                                     sc.buf("dbgatt"), reads=b_ATT))
        sc.barrier()
        if _STOP_AFTER == 2:
            early_finish()

        A.reset(R1_END)
        h2T = A.bf(8 * (TOWN + 8)).rearrange("p (k n) -> p k n", k=8)
        H2_END = A.off
        b_h2T = [sc.buf() for _ in range(16)]
        b_h2h = sc.buf("h2halo")
        woutb = A.bf(8 * D).rearrange("p (k n) -> p k n", k=8)
        b_wout = sc.buf("wout")
        wout_v = wout_d.ap().rearrange("(k p) n -> p k n", p=128)
        sc.dma("pool", lambda e, s: [e.dma_start(out=woutb[:, k, :], in_=wout_v[:, k, :]).then_inc(s, 16) for k in range(8)],
               b_wout, writes=[b_wout], n=8)
        xt = [A.f32(D) for _ in range(2)]
        x1t = [A.f32(D) for _ in range(2)]
        tt_ = [A.f32(D) for _ in range(2)]
        h2tok = [A.bf(D) for _ in range(2)]
        attT = [A.bf(512).rearrange("p (h n) -> p h n", h=4) for _ in range(2)]
        junk = A.f32(D)
        osm = A.f32(32).rearrange("p (a b) -> p a b", a=2)
        b_xt, b_x1t, b_tt, b_h2tok, b_attT = ([sc.buf(), sc.buf()] for _ in range(5))
        b_junk = sc.buf()
        b_osm = [sc.buf(), sc.buf()]
        x1_toks = []
        hal_toks = []
        halloc_v = halloc.ap()
        for t in range(16):
            par = t % 2
            jblk = t // 4
            sc.dma("sp", (lambda e, s, t=t, par=par: e.dma_start(out=xt[par], in_=xtok_d[t * 128:(t + 1) * 128, :]).then_inc(s, 16)),
                   b_xt[par], writes=[b_xt[par]])
            bk = nb()

            def fn_tr(e, t=t, bk=bk):
                ins = None
                for h in range(NH):
                    ins = e.transpose(out=psbf[:, bk * 1024 + h * 128:bk * 1024 + (h + 1) * 128], in_=ATT[:, t, h * 128:(h + 1) * 128], identity=ident)
                return ins
            sc.op("pe", fn_tr, reads=[b_ATT[t], b_cbf], writes=[pbuf[bk]])
            sc.op("dve", (lambda e, bk=bk, par=par: e.tensor_copy(out=attT[par][:, :, :],
                                                                  in_=psbf[:, bk * 1024:bk * 1024 + 512].rearrange("p (h n) -> p h n", h=4))),
                  reads=[pbuf[bk]], writes=[b_attT[par]])
            bks = [nb(), nb()]
            for nh in range(2):
                lhs = [attT[par][:, kc, :] for kc in range(4)] + [CT[:, m, t * 128:(t + 1) * 128] for m in range(4)]
                rhs = [woutb[:, kc, nh * 512:(nh + 1) * 512] for kc in range(8)]
                mm_group(bks[nh], lhs, rhs, 512, reads=[b_attT[par], b_CT[jblk], b_wout])
            o = osm[:, par, :]
            for nh in range(2):
                sc.op("act", (lambda e, nh=nh, o=o, bks=bks: e.activation(out=junk[:, 0:512], in_=bank(bks[nh]), func=AF.Square,
                                                                          accum_out=o[:, nh:nh + 1])),
                      reads=[pbuf[bks[nh]]], writes=[b_junk, b_osm[par]])
            sc.op("dve", (lambda e, o=o: e.tensor_tensor(out=o[:, 2:3], in0=o[:, 0:1], in1=o[:, 1:2], op=ALU.add)), reads=[b_osm[par]], writes=[b_osm[par]])
            sc.op("act", (lambda e, o=o: e.activation(out=o[:, 4:5], in_=o[:, 2:3], func=AF.Sqrt, scale=1.0 / D, bias=epsb)), reads=[b_osm[par], b_eps], writes=[b_osm[par]])
            sc.op("dve", (lambda e, o=o: e.reciprocal(out=o[:, 4:5], in_=o[:, 4:5])), reads=[b_osm[par]], writes=[b_osm[par]])
            for nh in range(2):
                sc.op("dve", (lambda e, nh=nh, o=o, bks=bks, par=par: e.scalar_tensor_tensor(
                    out=tt_[par][:, nh * 512:(nh + 1) * 512], in0=bank(bks[nh]), scalar=o[:, 4:5], in1=rows[:, R_GP1 + nh * 512:R_GP1 + (nh + 1) * 512],
                    op0=ALU.mult, op1=ALU.mult)), reads=[pbuf[bks[nh]], b_osm[par], b_rows], writes=([b_tt[par]] if nh == 0 else []),
                    sets=([b_tt[par]] if nh == 1 else []))
            sc.op("pool", (lambda e, par=par: e.tensor_tensor(out=x1t[par], in0=tt_[par], in1=xt[par], op=ALU.add)),
                  reads=[b_tt[par], b_xt[par]], writes=[b_x1t[par]])
            x1_toks.append(sc.dma("sp", (lambda e, s, t=t, par=par: e.dma_start(out=x1_d[t * 128:(t + 1) * 128, :], in_=x1t[par]).then_inc(s, 16)),
                                  b_x1t[par], reads=[b_x1t[par]]))
            if debug & 4:
                final_toks.append(sc.dma("sp", (lambda e, s, t=t, par=par: e.dma_start(out=dbg["x1"][t * 128:(t + 1) * 128, :], in_=x1t[par]).then_inc(s, 16)),
                                         b_x1t[par], reads=[b_x1t[par]]))
            sc.op("act", (lambda e, par=par, o=o: e.activation(out=junk, in_=x1t[par], func=AF.Square, accum_out=o[:, 5:6])),
                  reads=[b_x1t[par]], writes=[b_junk, b_osm[par]])
            sc.op("act", (lambda e, o=o: e.activation(out=o[:, 7:8], in_=o[:, 5:6], func=AF.Sqrt, scale=1.0 / D, bias=epsb)), reads=[b_osm[par], b_eps], writes=[b_osm[par]])
            sc.op("dve", (lambda e, o=o: e.reciprocal(out=o[:, 7:8], in_=o[:, 7:8])), reads=[b_osm[par]], writes=[b_osm[par]])
            sc.op("dve", (lambda e, par=par, o=o: e.tensor_scalar(out=h2tok[par], in0=x1t[par], scalar1=o[:, 7:8], scalar2=None, op0=ALU.mult)),
                  reads=[b_x1t[par], b_osm[par]], writes=[b_h2tok[par]])
            if t % 4 == 3:
                hal_toks.append(sc.dma("sp", (lambda e, s, par=par, jblk=jblk: e.dma_start(out=halloc_v[2 * jblk:2 * jblk + 2, :],
                                                                                             in_=h2tok[par][126:128, :]).then_inc(s, 16)),
                                       b_h2tok[par], reads=[b_h2tok[par]]))
            bk2 = nb()

            def fn_tr2(e, par=par, bk2=bk2):
                ins = None
                for kc in range(8):
                    ins = e.transpose(out=psbf[:, bk2 * 1024 + kc * 128:bk2 * 1024 + (kc + 1) * 128], in_=h2tok[par][:, kc * 128:(kc + 1) * 128],
                                      identity=ident)
                return ins
            sc.op("pe", fn_tr2, reads=[b_h2tok[par], b_cbf], writes=[pbuf[bk2]])
            for kc in range(8):
                sc.op("act", (lambda e, kc=kc, t=t, bk2=bk2: e.activation(out=h2T[:, kc, t * 128:(t + 1) * 128],
                                                                         in_=psbf[:, bk2 * 1024 + kc * 128:bk2 * 1024 + (kc + 1) * 128],
                                                                         func=AF.Copy, scale=vecs[:, V_G2 + kc:V_G2 + kc + 1])),
                      reads=[pbuf[bk2], b_vecs], writes=([b_h2T[t]] if kc == 0 else []), sets=([b_h2T[t]] if kc == 7 else []))
        cc2 = stack.enter_context(nc.semaphore("cc2"))
        sc.sems["cc2"] = cc2
        sc.wait_all("pool", hal_toks)

        def fn_ag2(e):
            ins = e.collective_compute("AllGather", ALU.bypass, replica_groups=[list(range(NCORES))],
                                       ins=[halloc.ap()], outs=[halall.ap()])
            ins.then_inc(cc2, 1)
            return ins
        if not sc.disabled:
            sc.prog["pool"].append(("c", fn_ag2, "cc2"))
        ag2_tok = sc.op("pool", lambda e: e.memset(smalls[:, 7:8], 0.0), writes=[sc.buf("agflag2")])
        sc.barrier()

        if _STOP_AFTER == 3:
            early_finish()
        A.reset(0)
        YT = A.bf(NPAIR * 1024).rearrange("p (i n) -> p i n", i=NPAIR)
        halsb = A.bf(D)
        selsb = A.bf(8)
        assert A.off <= R1_END
        A.reset(max(A.off, H2_END))
        wdnb = A.bf(NPAIR * D).rearrange("p (i n) -> p i n", i=NPAIR)
        b_wdn = sc.buf("wdn")
        wdn_v = wdn_d.ap().rearrange("(i p) n -> p i n", p=128)
        sc.dma("pool", lambda e, s: [e.dma_start(out=wdnb[:, 2 * i:2 * i + 2, :], in_=wdn_v[:, 2 * i:2 * i + 2, :]).then_inc(s, 16) for i in range(11)],
               b_wdn, writes=[b_wdn], n=11)
        b_hal = sc.buf("halsb")
        b_sel = sc.buf("selsb")
        sc.dma("sp", lambda e, s: e.dma_start(out=halsb[0:64, :], in_=halall.ap()[:, :]).then_inc(s, 16), b_hal, writes=[b_hal], after=[ag2_tok])
        sc.dma("sp", lambda e, s: e.dma_start(out=selsb[0:64, :], in_=sel_d[:, :]).then_inc(s, 16), b_sel, writes=[b_sel])
        bkx = nb()

        def fn_sel(e, bkx=bkx):
            ins = None
            for kc in range(8):
                ins = e.matmul(bank(bkx, 8, kc * 8), lhsT=halsb[0:64, kc * 128:(kc + 1) * 128], rhs=selsb[0:64, :], start=(kc == 0), stop=(kc == 7),
                               skip_group_check=True)
            return ins
        sc.op("pe", fn_sel, reads=[b_hal, b_sel], writes=[pbuf[bkx]])
        for kc in range(8):
            sc.op("act", (lambda e, kc=kc, bkx=bkx: e.activation(out=h2T[:, kc, TOWN:TOWN + 8], in_=bank(bkx, 8, kc * 8), func=AF.Copy,
                                                                 scale=vecs[:, V_G2 + kc:V_G2 + kc + 1])),
                  reads=[pbuf[bkx], b_vecs], writes=([b_h2h] if kc == 0 else []), sets=([b_h2h] if kc == 7 else []))
        NW = 4
        wups = [A.bf(8 * 256).rearrange("p (k n) -> p k n", k=8) for _ in range(NW)]
        b_wup = [sc.buf("wup%d" % i) for i in range(NW)]
        wup_v = wup_d.ap().rearrange("(k p) n -> p k n", p=128)
        NU = 3
        ugu = [A.f32(2 * (BLK + 2)).rearrange("p (g n) -> p g n", g=2) for _ in range(NU)]
        ga_ = [A.f32(BLK) for _ in range(NU)]
        gu_ = [A.f32(BLK) for _ in range(NU)]
        gsq = [A.f32(BLK) for _ in range(2)]
        gp = [A.f32(BLK) for _ in range(2)]
        gsg = [A.f32(BLK) for _ in range(2)]
        b_ugu, b_ga, b_gu = ([sc.buf() for _ in range(NU)] for _ in range(3))
        b_gsq, b_gp, b_gsg = ([sc.buf(), sc.buf()] for _ in range(3))
        x1r = [A.f32(D) for _ in range(2)]
        ost = [A.f32(D) for _ in range(2)]
        fjunk = gsq[0]
        b_fjunk = b_gsq[0]
        fsm = A.f32(16).rearrange("p (a b) -> p a b", a=2)
        b_x1r, b_ost, b_fsm = ([sc.buf(), sc.buf()] for _ in range(3))
        b_YT = [[sc.buf() for _ in range(2)] for _ in range(NPAIR)]
        wctr = [0]
        slot_of_pair = {}

        def fw(ch, k):
            return vecs[:, V_FW + ch * 3 + k:V_FW + ch * 3 + k + 1]

        def fb(ch):
            return vecs[:, V_FB + ch:V_FB + ch + 1]

        def issue_w(hh, i):
            ws = wctr[0] % NW
            wctr[0] += 1
            slot_of_pair[(hh, i)] = ws

            def fn_w(e, s, ws=ws, i=i):
                e.dma_start(out=wups[ws][:, :, 0:128], in_=wup_v[:, :, i * 128:(i + 1) * 128]).then_inc(s, 16)
                e.dma_start(out=wups[ws][:, :, 128:256], in_=wup_v[:, :, DFF + i * 128:DFF + (i + 1) * 128]).then_inc(s, 16)
            sc.dma("pool", fn_w, b_wup[ws], writes=[b_wup[ws]], n=2)

        def stage_A(n, hh, i, bl):
            ws = slot_of_pair[(hh, i)]
            jb = hh * 2 + bl
            u = n % NU
            pr = n % 3
            bkg, bku, bkh2 = 2 * pr, 2 * pr + 1, 6 + (n % 2)
            hreads = [b_h2T[jb * 4 + q] for q in range(4)] + [b_wup[ws]]
            mm_group(bkg, [wups[ws][:, kc, 0:128] for kc in range(8)], [h2T[:, kc, jb * BLK:(jb + 1) * BLK] for kc in range(8)], 512, reads=hreads)
            mm_group(bku, [wups[ws][:, kc, 128:256] for kc in range(8)], [h2T[:, kc, jb * BLK:(jb + 1) * BLK] for kc in range(8)], 512, reads=hreads)

            def fn_hh(e, ws=ws, jb=jb, bkh2=bkh2):
                ins = None
                for g in range(2):
                    for kc in range(8):
                        ins = e.matmul(bank(bkh2, 2, g * 2), lhsT=wups[ws][:, kc, g * 128:(g + 1) * 128],
                                       rhs=h2T[:, kc, TOWN + 2 * jb:TOWN + 2 * jb + 2], start=(kc == 0 and g == 0), stop=(kc == 7),
                                       skip_group_check=True)
                return ins
            sc.op("pe", fn_hh, reads=[b_h2h, b_wup[ws]], writes=[pbuf[bkh2]])
            chg, chu = i, NPAIR + i
            src2 = psum[:, bkg * 512:bkg * 512 + 1024].rearrange("p (g n) -> p g n", g=2)
            sc.op("act", (lambda e, u=u, src2=src2: e.activation(out=ugu[u][:, :, 2:BLK + 2], in_=src2, func=AF.Copy)),
                  reads=[pbuf[bkg], pbuf[bku]], writes=[b_ugu[u]])
            sc.op("act", (lambda e, u=u, bkh2=bkh2: e.activation(out=ugu[u][:, :, 0:2], in_=bank(bkh2, 4).rearrange("p (g n) -> p g n", g=2),
                                                                func=AF.Copy)), reads=[pbuf[bkh2], b_ugu[u]], writes=[b_ugu[u]])
            sc.op("act", (lambda e, u=u, bkg=bkg, chg=chg: e.activation(out=ga_[u], in_=bank(bkg), func=AF.Identity,
                                                                        scale=fw(chg, 2), bias=fb(chg))),
                  reads=[pbuf[bkg], b_vecs], writes=[b_ga[u]])
            sc.op("act", (lambda e, u=u, bku=bku, chu=chu: e.activation(out=gu_[u], in_=bank(bku), func=AF.Identity,
                                                                        scale=fw(chu, 2), bias=fb(chu))),
                  reads=[pbuf[bku], b_vecs], writes=[b_gu[u]])

        def stage_B(n, i):
            u = n % NU
            chg, chu = i, NPAIR + i
            for k in (1, 0):
                sc.op("dve", (lambda e, u=u, k=k, chg=chg: e.scalar_tensor_tensor(out=ga_[u], in0=ugu[u][:, 0, k:k + BLK], scalar=fw(chg, k), in1=ga_[u],
                                                                                  op0=ALU.mult, op1=ALU.add)),
                      reads=[b_ugu[u], b_ga[u]], writes=[b_ga[u]])
                sc.op("dve", (lambda e, u=u, k=k, chu=chu: e.scalar_tensor_tensor(out=gu_[u], in0=ugu[u][:, 1, k:k + BLK], scalar=fw(chu, k), in1=gu_[u],
                                                                                  op0=ALU.mult, op1=ALU.add)),
                      reads=[b_ugu[u], b_gu[u]], writes=[b_gu[u]])

        def stage_C1(n):
            u, v = n % NU, n % 2
            sc.op("act", (lambda e, u=u, v=v: e.activation(out=gsq[v], in_=ga_[u], func=AF.Square)), reads=[b_ga[u]], writes=[b_gsq[v]])

        def stage_C1b(n):
            u, v = n % NU, n % 2
            sc.op("pool", (lambda e, v=v: e.tensor_scalar(out=gp[v], in0=gsq[v], scalar1=0.044715, scalar2=1.0, op0=ALU.mult, op1=ALU.add)),
                  reads=[b_gsq[v]], writes=[b_gp[v]])
            sc.op("dve", (lambda e, u=u, v=v: e.tensor_tensor(out=gp[v], in0=gp[v], in1=ga_[u], op=ALU.mult)), reads=[b_gp[v], b_ga[u]], writes=[b_gp[v]])

        def stage_C2a(n):
            u, v = n % NU, n % 2
            sc.op("act", (lambda e, v=v: e.activation(out=gsg[v], in_=gp[v], func=AF.Sigmoid, scale=1.5957691216057308)),
                  reads=[b_gp[v]], writes=[b_gsg[v]])
            sc.op("pool", (lambda e, u=u, v=v: e.tensor_tensor(out=gsg[v], in0=gsg[v], in1=ga_[u], op=ALU.mult)), reads=[b_gsg[v], b_ga[u]], writes=[b_gsg[v]])

        def stage_C2b(n, i, bl):
            u, v = n % NU, n % 2
            sc.op("dve", (lambda e, u=u, v=v, i=i, bl=bl: e.tensor_tensor(out=YT[:, i, bl * BLK:(bl + 1) * BLK], in0=gsg[v], in1=gu_[u], op=ALU.mult)),
                  reads=[b_gsg[v], b_gu[u]], writes=[b_YT[i][bl]])

        nglob = [0]
        for hh in range(2):
            unit_list = [(i, bl) for i in range(NPAIR) for bl in range(2)]
            NUN = len(unit_list)
            base = nglob[0]
            issue_w(hh, 0)
            issue_w(hh, 1)
            for t in range(NUN + 2):
                if t < NUN:
                    i, bl = unit_list[t]
                    if bl == 0 and i + 2 < NPAIR:
                        issue_w(hh, i + 2)
                    stage_A(base + t, hh, i, bl)
                if 1 <= t <= NUN:
                    stage_B(base + t - 1, unit_list[t - 1][0])
                if 2 <= t:
                    stage_C2a(base + t - 2)
                if 1 <= t <= NUN:
                    stage_C1(base + t - 1)
                if 2 <= t:
                    stage_C2b(base + t - 2, *unit_list[t - 2])
                if 1 <= t <= NUN:
                    stage_C1b(base + t - 1)
            nglob[0] += NUN
            for tl in range(8):
                t = hh * 8 + tl
                par = t % 2
                bl = tl // 4
                sc.dma("sp", (lambda e, s, t=t, par=par: e.dma_start(out=x1r[par], in_=x1_d[t * 128:(t + 1) * 128, :]).then_inc(s, 16)),
                       b_x1r[par], writes=[b_x1r[par]], after=[x1_toks[t]])
                bks = [nb(), nb()]
                for nh in range(2):
                    mm_group(bks[nh], [YT[:, i, tl * 128:(tl + 1) * 128] for i in range(NPAIR)],
                             [wdnb[:, i, nh * 512:(nh + 1) * 512] for i in range(NPAIR)], 512,
                             reads=[b_YT[i][bl] for i in range(NPAIR)] + [b_wdn])
                o = fsm[:, par, :]
                for nh in range(2):
                    sc.op("act", (lambda e, nh=nh, o=o, bks=bks: e.activation(out=fjunk, in_=bank(bks[nh]), func=AF.Square, accum_out=o[:, nh:nh + 1])),
                          reads=[pbuf[bks[nh]]], writes=[b_fjunk, b_fsm[par]])
                sc.op("dve", (lambda e, o=o: e.tensor_tensor(out=o[:, 2:3], in0=o[:, 0:1], in1=o[:, 1:2], op=ALU.add)), reads=[b_fsm[par]], writes=[b_fsm[par]])
                sc.op("act", (lambda e, o=o: e.activation(out=o[:, 4:5], in_=o[:, 2:3], func=AF.Sqrt, scale=1.0 / D, bias=epsb)), reads=[b_fsm[par], b_eps], writes=[b_fsm[par]])
                sc.op("dve", (lambda e, o=o: e.reciprocal(out=o[:, 4:5], in_=o[:, 4:5])), reads=[b_fsm[par]], writes=[b_fsm[par]])
                for nh in range(2):
                    sc.op("dve", (lambda e, nh=nh, o=o, bks=bks, par=par: e.scalar_tensor_tensor(
                        out=ost[par][:, nh * 512:(nh + 1) * 512], in0=bank(bks[nh]), scalar=o[:, 4:5],
                        in1=rows[:, R_GP2 + nh * 512:R_GP2 + (nh + 1) * 512], op0=ALU.mult, op1=ALU.mult)),
                        reads=[pbuf[bks[nh]], b_fsm[par], b_rows], writes=([b_ost[par]] if nh == 0 else []), sets=([b_ost[par]] if nh == 1 else []))
                sc.op("pool", (lambda e, par=par: e.tensor_tensor(out=ost[par], in0=ost[par], in1=x1r[par], op=ALU.add)),
                      reads=[b_ost[par], b_x1r[par]], writes=[b_ost[par]])
                final_toks.append(sc.dma("sp", (lambda e, s, t=t, par=par: e.dma_start(out=out_d[t * 128:(t + 1) * 128, :], in_=ost[par]).then_inc(s, 16)),
                                         b_ost[par], reads=[b_ost[par]]))
        sc.wait_all("sp", final_toks)

        def replay(eng, e):
            for it in sc.prog[eng]:
                if it[0] == "w":
                    e.wait_ge(sc.sems[it[1]], it[2])
                elif it[0] == "o":
                    it[1](e).then_inc(sc.sems[eng], 1)
                elif it[0] == "c":
                    it[1](e)
                    e.wait_ge(sc.sems[it[2]], 1)
                elif it[0] == "ct":
                    it[1](e)
                elif it[0] == "cw":
                    e.wait_ge(sc.sems[it[1]], 1)
                else:
                    it[1](e, sc.sems[it[2]])

        @block.tensor
        def _(e):
            replay("pe", e)

        @block.scalar
        def _(e):
            replay("act", e)

        @block.vector
        def _(e):
            replay("dve", e)

        @block.gpsimd
        def _(e):
            replay("pool", e)

        @block.sync
        def _(e):
            replay("sp", e)
    return nc


def _host_prep(inputs):
    f32 = np.float32
    x = np.asarray(inputs["x"], dtype=f32)[0]
    pos = np.asarray(inputs["positions"]).astype(np.int32)[0]
    w_in = np.asarray(inputs["w_in"], dtype=f32)[0]
    perm = np.arange(512).reshape(4, 2, 2, 32)[:, :, ::-1, :].reshape(512)
    q, k, v, cg = w_in[:, 0:512], w_in[:, 512:1024], w_in[:, 1024:1536], w_in[:, 1536:2560]
    win = np.ascontiguousarray(np.concatenate([q, q[:, perm], k, k[:, perm], v, cg], axis=1))
    wout = np.ascontiguousarray(np.asarray(inputs["w_out"], dtype=f32)[0])
    wup = np.ascontiguousarray(np.asarray(inputs["w_up"], dtype=f32)[0])
    wdn = np.ascontiguousarray(np.asarray(inputs["w_down"], dtype=f32)[0])
    vecs = np.zeros((128, NV), f32)
    vecs[:, V_G1:V_G1 + 8] = np.asarray(inputs["attn_pre_g"], f32)[0].reshape(8, 128).T
    vecs[:, V_G2:V_G2 + 8] = np.asarray(inputs["ffn_pre_g"], f32)[0].reshape(8, 128).T
    cw = np.asarray(inputs["conv_dw_w"], f32)[0]
    vecs[:, V_CW:V_CW + 124] = cw.reshape(31, 4, 128).transpose(2, 1, 0).reshape(128, 124)
    vecs[:, V_CB:V_CB + 4] = np.asarray(inputs["conv_dw_b"], f32)[0].reshape(4, 128).T
    vecs[:, V_LG:V_LG + 4] = np.asarray(inputs["conv_ln_g"], f32)[0].reshape(4, 128).T
    vecs[:, V_LB:V_LB + 4] = np.asarray(inputs["conv_ln_b"], f32)[0].reshape(4, 128).T
    fw = np.asarray(inputs["ffn_dw_w"], f32)[0]
    vecs[:, V_FW:V_FW + 132] = fw.reshape(3, 44, 128).transpose(2, 1, 0).reshape(128, 132)
    vecs[:, V_FB:V_FB + 44] = np.asarray(inputs["ffn_dw_b"], f32)[0].reshape(44, 128).T
    inv_freq = (np.float32(10000.0) ** (-np.arange(0, 64, 2, dtype=np.float32) / np.float32(64))).astype(f32)
    p = np.arange(128)
    vecs[:, V_IF] = inv_freq[p % 32]
    sign = np.where((p // 32) % 2 == 0, -1.0, 1.0).astype(f32)
    vecs[:, V_SG] = sign
    vecs[:, V_NPS] = (-np.float32(PI) * sign).astype(f32)
    rows = np.zeros((1, NR), f32)
    rows[0, R_GP1:R_GP1 + 1024] = np.asarray(inputs["attn_post_g"], f32)[0]
    rows[0, R_GP2:R_GP2 + 1024] = np.asarray(inputs["ffn_post_g"], f32)[0]
    rows[0, R_SUB:R_SUB + 128] = np.asarray(inputs["subln_g"], f32)[0]
    rows[0, R_LQ1:R_LQ1 + 64] = np.asarray(inputs["lambda_q1"], f32)[0]
    rows[0, R_LK1:R_LK1 + 64] = np.asarray(inputs["lambda_k1"], f32)[0]
    rows[0, R_LQ2:R_LQ2 + 64] = np.asarray(inputs["lambda_q2"], f32)[0]
    rows[0, R_LK2:R_LK2 + 64] = np.asarray(inputs["lambda_k2"], f32)[0]
    bf = ml_dtypes.bfloat16
    cbf = np.zeros((128, 256), bf)
    cbf[:, 0:128] = np.eye(128, dtype=f32).astype(bf)
    kk = np.arange(128)[:, None]
    qq = np.arange(512)[None, :]
    diag = np.stack([(qq >= kt * 128 + kk) for kt in range(4)], axis=1).astype(f32)
    ones = np.ones((128, 4, 512), f32)
    zeros = np.zeros((128, 4, 512), f32)
    in_maps = []
    for c in range(NCORES):
        blocks = [8 * j + c for j in range(NOWN)]
        xT = np.zeros((D, NOWN * XW), f32)
        for j, g in enumerate(blocks):
            if g > 0:
                xT[:, j * XW:j * XW + HALO] = x[g * BLK - HALO:g * BLK].T
            xT[:, j * XW + HALO:(j + 1) * XW] = x[g * BLK:(g + 1) * BLK].T
        xtok = np.ascontiguousarray(np.concatenate([x[g * BLK:(g + 1) * BLK] for g in blocks], axis=0))
        posc = np.ascontiguousarray(np.concatenate([pos[g * BLK:(g + 1) * BLK] for g in blocks])[None, :])
        m = np.zeros((NOWN, 8, 128, 4, 512), f32)
        for j in range(NOWN):
            for sl in range(8):
                m[j, sl] = ones if sl < c else (diag if sl == c else zeros)
        masks = m.reshape(NOWN * 8 * 128, 4 * 512).astype(bf)
        sel = np.zeros((64, 8), f32)
        for j in range(NOWN):
            g = 8 * j + c
            if g == 0:
                continue
            pr, pj = ((c - 1), j) if c >= 1 else (7, j - 1)
            for t in range(2):
                sel[pr * 8 + 2 * pj + t, 2 * j + t] = 1.0
        in_maps.append({"xT": xT, "xtok": xtok, "pos": posc, "win": win, "wout": wout, "wup": wup, "wdn": wdn,
                        "vecs": vecs, "rows": rows, "cbf": cbf, "masks": masks, "sel": sel.astype(bf)})
    return in_maps


_NC_CACHE = {}


def kernel(**inputs):
    in_maps = _host_prep(inputs)
    if "nc" not in _NC_CACHE:
        _NC_CACHE["nc"] = build(debug=0)
    nc = _NC_CACHE["nc"]
    res = run_bass_kernel_spmd(nc, in_maps, core_ids=list(range(NCORES)))
    out = np.zeros((S, D), np.float32)
    for c in range(NCORES):
        o = np.asarray(res.results[c]["out"], dtype=np.float32)
        for j in range(NOWN):
            g = 8 * j + c
            out[g * BLK:(g + 1) * BLK] = o[j * BLK:(j + 1) * BLK]
    return out[None]
```

```python
import math
from contextlib import ExitStack

import numpy as np
import ml_dtypes

import concourse.bass as bass
import concourse.mybir as mybir
from concourse.bass_utils import run_bass_kernel_spmd

F32 = mybir.dt.float32
BF16 = mybir.dt.bfloat16
I32 = mybir.dt.int32
ALU = mybir.AluOpType
AF = mybir.ActivationFunctionType

NCORES = 8
S = 16384
D = 1024
BLK = 512
NOWN = 4
TOWN = NOWN * BLK
HALO = 32
XW = HALO + BLK
NH = 4
DFF = 2816
NPAIR = DFF // 128
INX = 3584
EPS = 1e-6
LAMBDA_INIT = 0.8 - 0.6 * math.exp(-0.3 * 0)
PI = math.pi

V_G1 = 0
V_G2 = 8
V_CW = 16
V_CB = 140
V_LG = 144
V_LB = 148
V_FW = 152
V_FB = 284
V_IF = 328
V_SG = 329
V_NPS = 330
NV = 332
R_GP1 = 0
R_GP2 = 1024
R_SUB = 2048
R_LQ1 = 2176
R_LK1 = 2240
R_LQ2 = 2304
R_LK2 = 2368
NR = 2432

SAME_ENGINE_SYNC = True
_STEP_LIMIT = 1000
_STOP_AFTER = 0


class Buf:
    __slots__ = ("name", "last_w", "readers")

    def __init__(self, name):
        self.name = name
        self.last_w = None
        self.readers = []


class Sched:
    ENG = ("pe", "act", "dve", "pool", "sp")

    def __init__(self, nc, stack):
        self.nc = nc
        self.stack = stack
        self.sems = {e: stack.enter_context(nc.semaphore("s_" + e)) for e in self.ENG}
        self.cnt = {e: 0 for e in self.ENG}
        self.prog = {e: [] for e in self.ENG}
        self.waited = {e: {} for e in self.ENG}
        self.dcnt = {}
        self.nbuf = 0
        self.disabled = False

    def buf(self, name=None):
        self.nbuf += 1
        return Buf(name or ("b%d" % self.nbuf))

    def _need(self, eng, toks):
        best = {}
        for t in toks:
            if t is None:
                continue
            k, v = t
            if not SAME_ENGINE_SYNC and k == eng:
                continue
            if v > best.get(k, 0):
                best[k] = v
        w = self.waited[eng]
        for k, v in best.items():
            if w.get(k, 0) >= v:
                continue
            w[k] = v
            self.prog[eng].append(("w", k, v))

    @staticmethod
    def _deps(reads, writes, after):
        toks = [b.last_w for b in reads] + [b.last_w for b in writes]
        for b in writes:
            toks.extend(b.readers)
        toks.extend(after)
        return toks

    def op(self, eng, fn, reads=(), writes=(), sets=(), after=()):
        if self.disabled:
            return None
        self._need(eng, self._deps(reads, writes, after))
        self.cnt[eng] += 1
        tok = (eng, self.cnt[eng])
        self.prog[eng].append(("o", fn))
        for b in reads:
            b.readers.append(tok)
        for b in writes:
            b.last_w = tok
            b.readers = []
        for b in sets:
            b.last_w = tok
        return tok

    def dma(self, q, fn, owner, reads=(), writes=(), after=(), n=1):
        if self.disabled:
            return None
        self._need(q, self._deps(reads, writes, after))
        key = "d:" + owner.name
        if key not in self.sems:
            self.sems[key] = self.stack.enter_context(self.nc.semaphore("d_" + owner.name))
            self.dcnt[key] = 0
        self.dcnt[key] += 16 * n
        tok = (key, self.dcnt[key])
        self.prog[q].append(("d", fn, key))
        for b in reads:
            b.readers.append(tok)
        for b in writes:
            b.last_w = tok
            b.readers = []
        return tok

    def barrier(self):
        if self.disabled:
            return
        toks = [(e, self.cnt[e]) for e in self.ENG if self.cnt[e] > 0]
        toks += [(k, v) for k, v in self.dcnt.items()]
        for e in self.ENG:
            self._need(e, toks)

    def wait_all(self, eng, toks):
        if self.disabled:
            return
        self._need(eng, toks)

    def replay(self, eng, e):
        for it in self.prog[eng]:
            if it[0] == "w":
                e.wait_ge(self.sems[it[1]], it[2])
            elif it[0] == "o":
                ins = it[1](e)
                ins.then_inc(self.sems[eng], 1)
            else:
                it[1](e, self.sems[it[2]])


class Arena:
    def __init__(self, t32, nwords):
        self.t32 = t32
        self.tbf = t32[:, :].bitcast(BF16)
        self.nwords = nwords
        self.off = 0

    def reset(self, off=0):
        self.off = off

    def f32(self, n):
        o = self.off
        self.off += n
        assert self.off <= self.nwords, ("arena overflow", self.off, self.nwords)
        return self.t32[:, o:o + n]

    def bf(self, n):
        n2 = (n + 1) // 2
        o = self.off
        self.off += n2
        assert self.off <= self.nwords, ("arena overflow", self.off, self.nwords)
        return self.tbf[:, 2 * o:2 * o + n]


def build(debug=0):
    nc = bass.Bass("TRN2", target_bir_lowering=False)
    xT_d = nc.dram_tensor("xT", [D, NOWN * XW], F32, kind="ExternalInput")
    xtok_d = nc.dram_tensor("xtok", [TOWN, D], F32, kind="ExternalInput")
    pos_d = nc.dram_tensor("pos", [1, TOWN], I32, kind="ExternalInput")
    win_d = nc.dram_tensor("win", [D, INX], F32, kind="ExternalInput")
    wout_d = nc.dram_tensor("wout", [D, D], F32, kind="ExternalInput")
    wup_d = nc.dram_tensor("wup", [D, 2 * DFF], F32, kind="ExternalInput")
    wdn_d = nc.dram_tensor("wdn", [DFF, D], F32, kind="ExternalInput")
    vecs_d = nc.dram_tensor("vecs", [128, NV], F32, kind="ExternalInput")
    rows_d = nc.dram_tensor("rows", [1, NR], F32, kind="ExternalInput")
    cbf_d = nc.dram_tensor("cbf", [128, 256], BF16, kind="ExternalInput")
    mask_d = nc.dram_tensor("masks", [NOWN * 8 * 128, 4 * BLK], BF16, kind="ExternalInput")
    sel_d = nc.dram_tensor("sel", [64, 8], BF16, kind="ExternalInput")
    out_d = nc.dram_tensor("out", [TOWN, D], F32, kind="ExternalOutput")
    kvloc = nc.dram_tensor("kvloc", [4096, BLK], BF16)
    kvall = nc.dram_tensor("kvall", [NCORES * 4096, BLK], BF16)
    halloc = nc.dram_tensor("halloc", [8, D], BF16)
    halall = nc.dram_tensor("halall", [64, D], BF16)
    x1_d = nc.dram_tensor("x1s", [TOWN, D], F32)
    dbg = {}
    if debug:
        dbg["qt"] = nc.dram_tensor("dbg_qt", [128, NH * TOWN], BF16, kind="ExternalOutput")
        dbg["ct"] = nc.dram_tensor("dbg_ct", [128, 4 * TOWN], BF16, kind="ExternalOutput")
        dbg["kv"] = nc.dram_tensor("dbg_kv", [4096, BLK], BF16, kind="ExternalOutput")
        dbg["att"] = nc.dram_tensor("dbg_att", [128, 16 * 512], BF16, kind="ExternalOutput")
        dbg["x1"] = nc.dram_tensor("dbg_x1", [TOWN, D], F32, kind="ExternalOutput")

    AW = 49600
    with ExitStack() as stack:
        arena_t = stack.enter_context(nc.sbuf_tensor("arena", [128, AW], F32))
        vecs = stack.enter_context(nc.sbuf_tensor("vecs_sb", [128, NV], F32))
        rows = stack.enter_context(nc.sbuf_tensor("rows_sb", [128, NR], F32))
        cbf = stack.enter_context(nc.sbuf_tensor("cbf_sb", [128, 256], BF16))
        onesf = stack.enter_context(nc.sbuf_tensor("onesf", [128, 128], F32))
        smalls = stack.enter_context(nc.sbuf_tensor("smalls", [128, 80], F32))
        subg8 = stack.enter_context(nc.sbuf_tensor("subg8", [128, 128], F32))
        psum = stack.enter_context(nc.psum_tensor("ps", [128, 8 * 512], F32))
        block = stack.enter_context(nc.Block())
        sc = Sched(nc, stack)
        A = Arena(arena_t, AW)
        ident = cbf[:, 0:128]
        psbf = psum[:, :].bitcast(BF16)

        def bank(k, n=512, o=0):
            return psum[:, k * 512 + o:k * 512 + o + n]

        pbuf = [sc.buf("psb%d" % k) for k in range(8)]
        final_toks = []


        def early_finish():
            final_toks.append(sc.dma("sp", lambda e, s: e.dma_start(out=out_d[:, :], in_=xtok_d[:, :]).then_inc(s, 16), sc.buf("early")))
            sc.wait_all("sp", final_toks)
            sc.disabled = True
        b_vecs, b_rows, b_cbf, b_ones, b_small, b_subg = (sc.buf("vecs"), sc.buf("rows"), sc.buf("cbf"),
                                                          sc.buf("ones"), sc.buf("smalls"), sc.buf("subg"))
        sc.dma("sp", lambda e, s: e.dma_start(out=vecs[:, :], in_=vecs_d[:, :]).then_inc(s, 16), b_vecs, writes=[b_vecs])
        sc.dma("sp", lambda e, s: e.dma_start(out=rows[:, :], in_=rows_d[0:1, :].partition_broadcast(128)).then_inc(s, 16),
               b_rows, writes=[b_rows])
        sc.dma("sp", lambda e, s: e.dma_start(out=cbf[:, :], in_=cbf_d[:, :]).then_inc(s, 16), b_cbf, writes=[b_cbf])
        sc.op("dve", lambda e: e.memset(onesf[:, :], 1.0), writes=[b_ones])
        epsb = smalls[:, 6:7]
        b_eps = sc.buf("eps")
        sc.op("dve", lambda e: e.memset(epsb, EPS), writes=[b_eps])
        sc.op("dve", lambda e: e.tensor_tensor(out=smalls[:, 8:8 + 64], in0=rows[:, R_LQ1:R_LQ1 + 64],
                                               in1=rows[:, R_LK1:R_LK1 + 64], op=ALU.mult), reads=[b_rows], writes=[b_small])
        sc.op("dve", lambda e: e.reduce_sum(out=smalls[:, 2:3], in_=smalls[:, 8:8 + 64], axis=mybir.AxisListType.X),
              reads=[b_small], writes=[b_small])
        sc.op("dve", lambda e: e.tensor_tensor(out=smalls[:, 8:8 + 64], in0=rows[:, R_LQ2:R_LQ2 + 64],
                                               in1=rows[:, R_LK2:R_LK2 + 64], op=ALU.mult), reads=[b_rows, b_small], writes=[b_small])
        sc.op("dve", lambda e: e.reduce_sum(out=smalls[:, 3:4], in_=smalls[:, 8:8 + 64], axis=mybir.AxisListType.X),
              reads=[b_small], writes=[b_small])
        sc.op("act", lambda e: e.activation(out=smalls[:, 4:6], in_=smalls[:, 2:4], func=AF.Exp), reads=[b_small], writes=[b_small])
        sc.op("dve", lambda e: e.tensor_tensor(out=smalls[:, 0:1], in0=smalls[:, 4:5], in1=smalls[:, 5:6], op=ALU.subtract),
              reads=[b_small], writes=[b_small])
        sc.op("dve", lambda e: e.tensor_scalar(out=smalls[:, 1:2], in0=smalls[:, 0:1], scalar1=-1.0, scalar2=-LAMBDA_INIT,
                                               op0=ALU.mult, op1=ALU.add), reads=[b_small], writes=[b_small])
        nlam = smalls[:, 1:2]
        sc.op("dve", lambda e: e.tensor_scalar(out=subg8[:, :], in0=rows[:, R_SUB:R_SUB + 128], scalar1=1.0 - LAMBDA_INIT,
                                               scalar2=None, op0=ALU.mult), reads=[b_rows], writes=[b_subg])

        A.reset(0)
        QT = A.bf(NH * TOWN).rearrange("p (h t) -> p h t", h=NH)
        CT = A.bf(4 * TOWN).rearrange("p (m t) -> p m t", m=4)
        ATT = A.bf(16 * 512).rearrange("p (t e) -> p t e", t=16)
        R1_END = A.off
        b_QT = [[sc.buf() for _ in range(NOWN)] for _ in range(NH)]
        b_CT = [sc.buf() for _ in range(NOWN)]
        b_ATT = [sc.buf() for _ in range(16)]

        winb = A.bf(8 * INX).rearrange("p (k n) -> p k n", k=8)
        b_win = [sc.buf("win%d" % k) for k in range(8)]
        win_v = win_d.ap().rearrange("(k p) n -> p k n", p=128)
        for kc in range(8):
            sc.dma("pool", (lambda e, s, kc=kc: e.dma_start(out=winb[:, kc, :], in_=win_v[:, kc, :]).then_inc(s, 16)),
                   b_win[kc], writes=[b_win[kc]])
        off_x = A.off
        xTs = A.f32(8 * XW).rearrange("p (k n) -> p k n", k=8)
        sqs = [A.f32(XW) for _ in range(2)]
        rstd = A.f32(XW)
        end1 = A.off
        A.reset(off_x)
        csb = A.f32(4 * XW).rearrange("p (m n) -> p m n", m=4)
        acc = A.f32(4 * BLK).rearrange("p (m n) -> p m n", m=4)
        mean_sb = A.f32(BLK)
        lrstd = A.f32(BLK)
        sig = A.f32(BLK)
        sigh = A.f32(4 * HALO)
        A.reset(max(end1, A.off))
        hT_all = [A.bf(8 * XW).rearrange("p (k n) -> p k n", k=8) for _ in range(NOWN)]
        posi = A.f32(BLK).bitcast(I32)
        ang = A.f32(BLK)
        ang2 = A.f32(BLK)
        rk_f = A.f32(BLK)
        rk_i = A.f32(BLK).bitcast(I32)
        C1 = 6.28125
        C2 = 2 * PI - 6.28125
        COS = A.f32(BLK)
        SIN = A.f32(BLK)
        tmpA = [A.f32(BLK) for _ in range(4)]
        ktst = A.bf(NH * BLK).rearrange("p (h t) -> p h t", h=NH)
        vst = A.bf(4 * 512).rearrange("p (k n) -> p k n", k=4)
        b_xT, b_sq, b_hT_all, b_rstd = sc.buf("xT"), [sc.buf(), sc.buf()], [sc.buf("hT%d" % q) for q in range(NOWN)], sc.buf("rstd")
        b_pos, b_ang, b_ang2, b_cos, b_sin = sc.buf("posi"), sc.buf(), sc.buf(), sc.buf(), sc.buf()
        b_rk, b_rki = sc.buf(), sc.buf()
        b_tmpA = [sc.buf() for _ in range(4)]
        b_ktst, b_vst, b_csb = sc.buf("ktst"), sc.buf("vst"), [sc.buf() for _ in range(4)]
        b_csbh = sc.buf("csbh")
        b_acc = [sc.buf() for _ in range(4)]
        b_ysq, b_mean, b_lrstd, b_sig, b_sigh = sc.buf(), sc.buf(), sc.buf(), sc.buf(), sc.buf()
        xT_v = xT_d.ap().rearrange("(k p) n -> p k n", p=128)
        kvloc_v = kvloc.ap()
        kv_toks = []
        rr = [0]

        def nb():
            k = rr[0] % 8
            rr[0] += 1
            return k

        def mm_group(bk, lhs_list, rhs_list, n, o=0, reads=()):
            def fn(e, bk=bk, lhs_list=lhs_list, rhs_list=rhs_list, n=n, o=o):
                ins = None
                L = len(lhs_list)
                for i in range(L):
                    ins = e.matmul(bank(bk, n, o), lhsT=lhs_list[i], rhs=rhs_list[i], start=(i == 0), stop=(i == L - 1))
                return ins
            return sc.op("pe", fn, reads=list(reads), writes=[pbuf[bk]])

        for j in range(NOWN):
            c0 = j * XW
            hT = hT_all[j]
            b_hT = b_hT_all[j]
            sc.dma("sp", (lambda e, s, c0=c0: e.dma_start(out=xTs[:, :, :], in_=xT_v[:, :, c0:c0 + XW]).then_inc(s, 16)),
                   b_xT, writes=[b_xT])
            bk_a, bk_b = nb(), nb()
            for kc in range(8):
                sq = sqs[kc % 2]
                sc.op("act", (lambda e, sq=sq, kc=kc: e.activation(out=sq, in_=xTs[:, kc, :], func=AF.Square)),
                      reads=[b_xT], writes=[b_sq[kc % 2]])

                def fn(e, sq=sq, kc=kc, bk_a=bk_a, bk_b=bk_b):
                    e.matmul(bank(bk_a), lhsT=onesf[:, :], rhs=sq[:, HALO:XW], start=(kc == 0), stop=(kc == 7))
                    return e.matmul(bank(bk_b, HALO), lhsT=onesf[:, :], rhs=sq[:, 0:HALO], start=(kc == 0), stop=(kc == 7))
                sc.op("pe", fn, reads=[b_sq[kc % 2], b_ones],
                      writes=([pbuf[bk_a], pbuf[bk_b]] if kc == 0 else []), sets=([pbuf[bk_a], pbuf[bk_b]] if kc == 7 else []))
            sc.op("act", (lambda e, bk_a=bk_a: e.activation(out=rstd[:, HALO:XW], in_=bank(bk_a), func=AF.Sqrt, scale=1.0 / D, bias=epsb)),
                  reads=[pbuf[bk_a], b_eps], writes=[b_rstd])
            sc.op("act", (lambda e, bk_b=bk_b: e.activation(out=rstd[:, 0:HALO], in_=bank(bk_b, HALO), func=AF.Sqrt, scale=1.0 / D, bias=epsb)),
                  reads=[pbuf[bk_b], b_eps], writes=[b_rstd])
            sc.op("dve", lambda e: e.reciprocal(out=rstd[:, :], in_=rstd[:, :]), reads=[b_rstd], writes=[b_rstd])
            for kc in range(8):
                sc.op("dve", (lambda e, kc=kc, hT=hT: e.scalar_tensor_tensor(out=hT[:, kc, :], in0=xTs[:, kc, :],
                                                                      scalar=vecs[:, V_G1 + kc:V_G1 + kc + 1], in1=rstd[:, :],
                                                                      op0=ALU.mult, op1=ALU.mult)),
                      reads=[b_xT, b_rstd, b_vecs], writes=([b_hT] if kc == 0 else []), sets=([b_hT] if kc == 7 else []))
            sc.dma("sp", (lambda e, s, j=j: e.dma_start(out=posi, in_=pos_d[0:1, j * BLK:(j + 1) * BLK].partition_broadcast(128)).then_inc(s, 16)),
                   b_pos, writes=[b_pos])
            sc.op("dve", lambda e: e.tensor_copy(out=ang, in_=posi), reads=[b_pos], writes=[b_ang])
            sc.op("dve", lambda e: e.tensor_scalar(out=ang, in0=ang, scalar1=vecs[:, V_IF:V_IF + 1], scalar2=None, op0=ALU.mult),
                  reads=[b_ang, b_vecs], writes=[b_ang])
            sc.op("dve", lambda e: e.tensor_scalar(out=ang2, in0=ang, scalar1=0.5 * PI, scalar2=None, op0=ALU.add),
                  reads=[b_ang], writes=[b_ang2])
            for (av, bv) in ((ang, b_ang), (ang2, b_ang2)):
                sc.op("dve", (lambda e, av=av: e.tensor_scalar(out=rk_f, in0=av, scalar1=1.0 / (2 * PI), scalar2=None, op0=ALU.mult)),
                      reads=[bv], writes=[b_rk])
                sc.op("dve", lambda e: e.tensor_copy(out=rk_i, in_=rk_f), reads=[b_rk], writes=[b_rki])
                sc.op("dve", lambda e: e.tensor_copy(out=rk_f, in_=rk_i), reads=[b_rki], writes=[b_rk])
                sc.op("dve", (lambda e, av=av: e.scalar_tensor_tensor(out=av, in0=rk_f, scalar=-C1, in1=av, op0=ALU.mult, op1=ALU.add)),
                      reads=[b_rk, bv], writes=[bv])
                sc.op("dve", (lambda e, av=av: e.scalar_tensor_tensor(out=av, in0=rk_f, scalar=-C2, in1=av, op0=ALU.mult, op1=ALU.add)),
                      reads=[b_rk, bv], writes=[bv])
                sc.op("dve", (lambda e, av=av: e.tensor_scalar(out=rk_f, in0=av, scalar1=PI, scalar2=-2 * PI, op0=ALU.is_gt, op1=ALU.mult)),
                      reads=[bv, b_rk], writes=[b_rk])
                sc.op("dve", (lambda e, av=av: e.tensor_tensor(out=av, in0=av, in1=rk_f, op=ALU.add)), reads=[bv, b_rk], writes=[bv])
                sc.op("dve", (lambda e, av=av: e.tensor_scalar(out=av, in0=av, scalar1=-PI, scalar2=PI, op0=ALU.max, op1=ALU.min)),
                      reads=[bv], writes=[bv])
            sc.op("act", lambda e: e.activation(out=COS, in_=ang2, func=AF.Sin), reads=[b_ang2], writes=[b_cos])
            sc.op("act", lambda e: e.activation(out=SIN, in_=ang, func=AF.Sin, scale=vecs[:, V_SG:V_SG + 1]),
                  reads=[b_ang, b_vecs], writes=[b_sin])
            for which in range(2):
                for h in range(NH):
                    ca = (0 if which == 0 else 8) + h
                    cb = ca + 4
                    bka, bkb = nb(), nb()
                    mm_group(bka, [winb[:, kc, ca * 128:(ca + 1) * 128] for kc in range(8)], [hT[:, kc, HALO:XW] for kc in range(8)],
                             512, reads=[b_hT] + b_win)
                    mm_group(bkb, [winb[:, kc, cb * 128:(cb + 1) * 128] for kc in range(8)], [hT[:, kc, HALO:XW] for kc in range(8)],
                             512, reads=[b_hT] + b_win)
                    t1, t2 = (0, 1) if (h % 2 == 0) else (2, 3)
                    sc.op("dve", (lambda e, bka=bka, t1=t1: e.tensor_tensor(out=tmpA[t1], in0=bank(bka), in1=COS, op=ALU.mult)),
                          reads=[pbuf[bka], b_cos], writes=[b_tmpA[t1]])
                    sc.op("dve", (lambda e, bkb=bkb, t2=t2: e.tensor_tensor(out=tmpA[t2], in0=bank(bkb), in1=SIN, op=ALU.mult)),
                          reads=[pbuf[bkb], b_sin], writes=[b_tmpA[t2]])
                    if which == 0:
                        dst = QT[:, h, j * BLK:(j + 1) * BLK]
                        sc.op("pool", (lambda e, dst=dst, t1=t1, t2=t2: e.tensor_tensor(out=dst, in0=tmpA[t1], in1=tmpA[t2], op=ALU.add)),
                              reads=[b_tmpA[t1], b_tmpA[t2]], writes=[b_QT[h][j]])
                    else:
                        dst = ktst[:, h, :]
                        sc.op("pool", (lambda e, dst=dst, t1=t1, t2=t2: e.tensor_tensor(out=dst, in0=tmpA[t1], in1=tmpA[t2], op=ALU.add)),
                              reads=[b_tmpA[t1], b_tmpA[t2]], writes=([b_ktst] if h == 0 else []), sets=([b_ktst] if h == NH - 1 else []))
            for h in range(NH):
                r0 = 2048 + (h * 4 + j) * 128
                kv_toks.append(sc.dma("sp", (lambda e, s, h=h, r0=r0: e.dma_start(out=kvloc_v[r0:r0 + 128, :], in_=ktst[:, h, :]).then_inc(s, 16)),
                                      b_ktst, reads=[b_ktst]))
            for tt in range(4):
                bk = nb()
                mm_group(bk, [hT[:, kc, HALO + tt * 128:HALO + (tt + 1) * 128] for kc in range(8)],
                         [winb[:, kc, 16 * 128:20 * 128] for kc in range(8)], 512, reads=[b_hT] + b_win)
                sc.op("act", (lambda e, bk=bk, tt=tt: e.activation(out=vst[:, tt, :], in_=bank(bk), func=AF.Copy)),
                      reads=[pbuf[bk]], writes=([b_vst] if tt == 0 else []), sets=([b_vst] if tt == 3 else []))
            for h in range(NH):
                r0 = (h * 4 + j) * 128
                kv_toks.append(sc.dma("sp", (lambda e, s, h=h, r0=r0: e.dma_start(
                    out=kvloc_v[r0:r0 + 128, :].rearrange("p (k e) -> p k e", k=4),
                    in_=vst[:, :, h * 128:(h + 1) * 128]).then_inc(s, 16)), b_vst, reads=[b_vst]))
        sc.barrier()
        b_kvall = sc.buf("kvall")
        cc_sem = stack.enter_context(nc.semaphore("cc1"))
        sc.sems["cc1"] = cc_sem
        sc.wait_all("pool", kv_toks)

        def fn_ag(e):
            ins = e.collective_compute("AllGather", ALU.bypass, replica_groups=[list(range(NCORES))],
                                       ins=[kvloc.ap()], outs=[kvall.ap()])
            ins.then_inc(cc_sem, 1)
            return ins
        sc.prog["pool"].append(("c", fn_ag, "cc1"))
        ag_tok = sc.op("pool", lambda e: e.memset(smalls[:, 7:8], 0.0), writes=[sc.buf("agflag1")])
        for j in range(NOWN):
            hT = hT_all[j]
            b_hT = b_hT_all[j]
            bkh = nb()
            for m in range(4):
                ca, cb = 20 + m, 24 + m
                def fnh(e, m=m, ca=ca, cb=cb, bkh=bkh, hT=hT):
                    ins = None
                    for (cc, oo) in ((ca, m * 64), (cb, m * 64 + 32)):
                        for kc in range(8):
                            ins = e.matmul(bank(bkh, HALO, oo), lhsT=winb[:, kc, cc * 128:(cc + 1) * 128], rhs=hT[:, kc, 0:HALO],
                                           start=(kc == 0 and m == 0 and cc == ca), stop=(kc == 7), skip_group_check=True)
                    return ins
                sc.op("pe", fnh, reads=[b_hT] + b_win, writes=([pbuf[bkh]] if m == 0 else []), sets=([pbuf[bkh]] if m == 3 else []))
            hv = bank(bkh, 256).rearrange("p (m two n) -> p m two n", m=4, two=2)
            sc.op("act", (lambda e, hv=hv: e.activation(out=sigh.rearrange("p (m n) -> p m n", m=4), in_=hv[:, :, 1, :], func=AF.Sigmoid)),
                  reads=[pbuf[bkh]], writes=[b_sigh])
            sc.op("dve", (lambda e, hv=hv: e.tensor_tensor(out=csb[:, :, 0:HALO], in0=hv[:, :, 0, :],
                                                           in1=sigh.rearrange("p (m n) -> p m n", m=4), op=ALU.mult)),
                  reads=[pbuf[bkh], b_sigh], writes=[b_csbh])
            for m in range(4):
                ca, cb = 20 + m, 24 + m
                bka, bkb = nb(), nb()
                mm_group(bka, [winb[:, kc, ca * 128:(ca + 1) * 128] for kc in range(8)], [hT[:, kc, HALO:XW] for kc in range(8)],
                         512, reads=[b_hT] + b_win)
                mm_group(bkb, [winb[:, kc, cb * 128:(cb + 1) * 128] for kc in range(8)], [hT[:, kc, HALO:XW] for kc in range(8)],
                         512, reads=[b_hT] + b_win)
                sc.op("act", (lambda e, bkb=bkb: e.activation(out=sig, in_=bank(bkb), func=AF.Sigmoid)), reads=[pbuf[bkb]], writes=[b_sig])
                sc.op("dve", (lambda e, bka=bka, m=m: e.tensor_tensor(out=csb[:, m, HALO:XW], in0=bank(bka), in1=sig, op=ALU.mult)),
                      reads=[pbuf[bka], b_sig], writes=[b_csb[m]])
            for k in range(31):
                for m in range(4):
                    wk = vecs[:, V_CW + m * 31 + k:V_CW + m * 31 + k + 1]
                    src = csb[:, m, 2 + k:2 + k + BLK]
                    if k == 0:
                        sc.op("dve", (lambda e, m=m, wk=wk, src=src: e.tensor_scalar(out=acc[:, m, :], in0=src, scalar1=wk,
                                                                                     scalar2=vecs[:, V_CB + m:V_CB + m + 1],
                                                                                     op0=ALU.mult, op1=ALU.add)),
                              reads=[b_csb[m], b_csbh, b_vecs], writes=[b_acc[m]])
                    else:
                        sc.op("dve", (lambda e, m=m, wk=wk, src=src: e.scalar_tensor_tensor(out=acc[:, m, :], in0=src, scalar=wk,
                                                                                            in1=acc[:, m, :], op0=ALU.mult, op1=ALU.add)),
                              reads=[b_csb[m], b_csbh, b_acc[m]], writes=[b_acc[m]])
            bkm, bke = nb(), nb()
            for m in range(4):
                sc.op("act", (lambda e, m=m: e.activation(out=tmpA[m], in_=acc[:, m, :], func=AF.Square)),
                      reads=[b_acc[m]], writes=[b_tmpA[m]])
            mm_group(bkm, [onesf[:, :]] * 4, [acc[:, m, :] for m in range(4)], 512, reads=b_acc + [b_ones])
            mm_group(bke, [onesf[:, :]] * 4, [tmpA[m] for m in range(4)], 512, reads=b_tmpA + [b_ones])
            sc.op("act", (lambda e, bkm=bkm: e.activation(out=mean_sb, in_=bank(bkm), func=AF.Copy, scale=1.0 / 512)),
                  reads=[pbuf[bkm]], writes=[b_mean])
            sc.op("dve", lambda e: e.tensor_tensor(out=lrstd, in0=mean_sb, in1=mean_sb, op=ALU.mult), reads=[b_mean], writes=[b_lrstd])
            sc.op("dve", (lambda e, bke=bke: e.scalar_tensor_tensor(out=lrstd, in0=bank(bke), scalar=1.0 / 512, in1=lrstd,
                                                                    op0=ALU.mult, op1=ALU.subtract)),
                  reads=[pbuf[bke], b_lrstd], writes=[b_lrstd])
            sc.op("act", lambda e: e.activation(out=lrstd, in_=lrstd, func=AF.Sqrt, bias=epsb), reads=[b_lrstd, b_eps], writes=[b_lrstd])
            sc.op("dve", lambda e: e.reciprocal(out=lrstd, in_=lrstd), reads=[b_lrstd], writes=[b_lrstd])
            for m in range(4):
                sc.op("dve", (lambda e, m=m: e.tensor_tensor(out=acc[:, m, :], in0=acc[:, m, :], in1=mean_sb, op=ALU.subtract)),
                      reads=[b_acc[m], b_mean], writes=[b_acc[m]])
                sc.op("dve", (lambda e, m=m: e.tensor_tensor(out=acc[:, m, :], in0=acc[:, m, :], in1=lrstd, op=ALU.mult)),
                      reads=[b_acc[m], b_lrstd], writes=[b_acc[m]])
                sc.op("act", (lambda e, m=m, j=j: e.activation(out=CT[:, m, j * BLK:(j + 1) * BLK], in_=acc[:, m, :], func=AF.Silu,
                                                              scale=vecs[:, V_LG + m:V_LG + m + 1], bias=vecs[:, V_LB + m:V_LB + m + 1])),
                      reads=[b_acc[m], b_vecs], writes=([b_CT[j]] if m == 0 else []), sets=([b_CT[j]] if m == 3 else []))

        if debug & 1:
            final_toks.append(sc.dma("sp", lambda e, s: e.dma_start(out=dbg["kv"][:, :], in_=kvloc_v[:, :]).then_inc(s, 16),
                                     sc.buf("dbgkv"), after=kv_toks))
            bq = sc.buf("dbgqt")
            final_toks.append(sc.dma("sp", lambda e, s: e.dma_start(out=dbg["qt"][:, :], in_=QT.rearrange("p h t -> p (h t)")).then_inc(s, 16),
                                     bq, reads=[b for hh in b_QT for b in hh]))
            final_toks.append(sc.dma("sp", lambda e, s: e.dma_start(out=dbg["ct"][:, :], in_=CT.rearrange("p m t -> p (m t)")).then_inc(s, 16),
                                     bq, reads=b_CT))
        sc.barrier()

        if _STOP_AFTER == 1:
            early_finish()
        A.reset(R1_END)
        NSLOT = 4
        kslot = [A.bf(BLK) for _ in range(NSLOT)]
        vslot = [A.bf(4 * 130).rearrange("p (k n) -> p k n", k=4) for _ in range(NSLOT)]
        b_slot = [sc.buf("kvs%d" % i) for i in range(NSLOT)]
        NP = 4
        Pb = [A.bf(2 * BLK).rearrange("p (c n) -> p c n", c=2) for _ in range(NP)]
        b_P = [sc.buf("P%d" % i) for i in range(NP)]
        msk = A.bf(8 * 4 * BLK).rearrange("p (s k n) -> p s k n", s=8, k=4)
        b_msk = sc.buf("msk")
        Osb = A.f32(4 * 512).rearrange("p (q n) -> p q n", q=4)
        b_Osb = [sc.buf() for _ in range(4)]
        fa = [A.f32(128) for _ in range(2)]
        fd = [A.f32(128) for _ in range(2)]
        fj = A.f32(128)
        dbuf = A.f32(4 * 128).rearrange("p (q n) -> p q n", q=4)
        fs4 = A.f32(32)
        b_dbuf = [sc.buf() for _ in range(4)]
        b_fs4 = sc.buf("fs4")
        fs = A.f32(16).rearrange("p (a b) -> p a b", a=2)
        b_fa, b_fd, b_fs = [sc.buf(), sc.buf()], [sc.buf(), sc.buf()], [sc.buf(), sc.buf()]
        b_fj = sc.buf()
        for i in range(NSLOT):
            sc.op("dve", (lambda e, i=i: e.memset(vslot[i][:, :, 128:130], 1.0)), writes=[b_slot[i]])
        SB = [(0, 1), (2, 3)]
        b_S = [sc.buf("S0"), sc.buf("S1")]
        OB = [4, 5, 6, 7]
        b_O = sc.buf("O")
        mask_v = mask_d.ap().rearrange("(j s p) (k n) -> j p s k n", j=NOWN, s=8, k=4)
        kvall_v = kvall.ap()
        load_ctr = [0]
        s_ctr = [0]
        p_ctr = [0]

        for j in range(NOWN):
            sc.dma("sp", (lambda e, s, j=j: [e.dma_start(out=msk[:, sl, :, :], in_=mask_v[j, :, sl, :, :]).then_inc(s, 16) for sl in range(8)]),
                   b_msk, writes=[b_msk], n=8)
            nsteps = min(8 * j + 8, _STEP_LIMIT)
            for h in range(NH):
                items = []
                for st in range(nsteps):
                    for kt in range(4):
                        items.append((st, kt))
                slot_of = {}

                def issue_load(st, h=h):
                    sl = load_ctr[0] % NSLOT
                    load_ctr[0] += 1
                    slot_of[st] = sl
                    r, jl = st % 8, st // 8
                    rv = r * 4096 + (h * 4 + jl) * 128
                    rk = rv + 2048

                    def fn(e, s, sl=sl, rv=rv, rk=rk):
                        e.dma_start(out=kslot[sl], in_=kvall_v[rk:rk + 128, :]).then_inc(s, 16)
                        e.dma_start(out=vslot[sl][:, :, 0:128], in_=kvall_v[rv:rv + 128, :].rearrange("p (k e) -> p k e", k=4)).then_inc(s, 16)
                    sc.dma("sp", fn, b_slot[sl], writes=[b_slot[sl]], after=[ag_tok], n=2)

                def issue_S(st, kt, h=h, j=j):
                    sb = s_ctr[0] % 2
                    s_ctr[0] += 1
                    sl = slot_of[st]
                    b0, b1 = SB[sb]

                    def fn(e, sl=sl, kt=kt, b0=b0, b1=b1):
                        e.matmul(bank(b0), lhsT=kslot[sl][0:64, kt * 128:(kt + 1) * 128], rhs=QT[0:64, h, j * BLK:(j + 1) * BLK],
                                 start=True, stop=True)
                        return e.matmul(bank(b1), lhsT=kslot[sl][64:128, kt * 128:(kt + 1) * 128], rhs=QT[64:128, h, j * BLK:(j + 1) * BLK],
                                        start=True, stop=True)
                    sc.op("pe", fn, reads=[b_slot[sl], b_QT[h][j]], writes=[b_S[sb]])
                    return sb

                def issue_exp(st, kt, sb):
                    pb = p_ctr[0] % NP
                    p_ctr[0] += 1
                    b0 = SB[sb][0]
                    src = psum[:, b0 * 512:b0 * 512 + 1024].rearrange("p (c n) -> p c n", c=2)
                    sc.op("act", (lambda e, pb=pb, src=src: e.activation(out=Pb[pb][:, :, :], in_=src, func=AF.Exp, scale=0.125)),
                          reads=[b_S[sb]], writes=[b_P[pb]])
                    if st >= 8 * j:
                        sl8 = st - 8 * j
                        for c in range(2):
                            sc.op("dve", (lambda e, pb=pb, c=c, sl8=sl8, kt=kt: e.tensor_tensor(out=Pb[pb][:, c, :], in0=Pb[pb][:, c, :],
                                                                                              in1=msk[:, sl8, kt, :], op=ALU.mult)),
                                  reads=[b_P[pb], b_msk], writes=[b_P[pb]])
                    return pb

                def issue_PV(st, kt, pb, first, last):
                    sl = slot_of[st]

                    def fn(e, sl=sl, kt=kt, pb=pb, first=first):
                        ins = None
                        for qc in range(4):
                            for c in range(2):
                                ins = e.matmul(bank(OB[qc], 129, c * 256), lhsT=Pb[pb][:, c, qc * 128:(qc + 1) * 128],
                                               rhs=vslot[sl][:, kt, 0:129], start=(first and c == 0), stop=last,
                                               skip_group_check=True)
                        return ins
                    sc.op("pe", fn, reads=[b_P[pb], b_slot[sl]], writes=([b_O] if first else []), sets=([b_O] if last else []))

                issue_load(0)
                if nsteps > 1:
                    issue_load(1)
                pend = None
                n_it = len(items)
                sb_next = issue_S(*items[0])
                pend = []
                for idx in range(n_it):
                    st, kt = items[idx]
                    pb = issue_exp(st, kt, sb_next)
                    pend.append((st, kt, pb, idx))
                    if idx + 1 < n_it:
                        st2, kt2 = items[idx + 1]
                        if kt2 == 0 and st2 + 1 < nsteps:
                            issue_load(st2 + 1)
                        sb_next = issue_S(st2, kt2)
                    if len(pend) > 1:
                        pst, pkt, ppb, pidx = pend.pop(0)
                        issue_PV(pst, pkt, ppb, first=(pidx == 0), last=False)
                while pend:
                    pst, pkt, ppb, pidx = pend.pop(0)
                    issue_PV(pst, pkt, ppb, first=(pidx == 0), last=(len(pend) == 0))
                for qc in range(4):
                    sc.op("dve", (lambda e, qc=qc: e.tensor_copy(out=Osb[:, qc, 0:385], in_=bank(OB[qc], 385))),
                          reads=[b_O], writes=[b_Osb[qc]])
                for qc in range(4):
                    a, ba = fa[qc % 2], b_fa[qc % 2]
                    sc.op("dve", (lambda e, qc=qc: e.reciprocal(out=fs4[:, qc:qc + 1], in_=Osb[:, qc, 128:129])), reads=[b_Osb[qc]], writes=[b_fs4])
                    sc.op("dve", (lambda e, qc=qc: e.reciprocal(out=fs4[:, 4 + qc:5 + qc], in_=Osb[:, qc, 384:385])), reads=[b_Osb[qc], b_fs4], writes=[b_fs4])
                    sc.op("dve", (lambda e, qc=qc: e.tensor_tensor(out=fs4[:, 8 + qc:9 + qc], in0=fs4[:, 4 + qc:5 + qc], in1=nlam, op=ALU.mult)),
                          reads=[b_fs4, b_small], writes=[b_fs4])
                    sc.op("dve", (lambda e, qc=qc, a=a: e.tensor_scalar(out=a, in0=Osb[:, qc, 0:128], scalar1=fs4[:, qc:qc + 1], scalar2=None,
                                                                         op0=ALU.mult)), reads=[b_Osb[qc], b_fs4], writes=[ba])
                    sc.op("dve", (lambda e, qc=qc, a=a: e.scalar_tensor_tensor(out=dbuf[:, qc, :], in0=Osb[:, qc, 256:384], scalar=fs4[:, 8 + qc:9 + qc], in1=a,
                                                                                op0=ALU.mult, op1=ALU.add)),
                          reads=[b_Osb[qc], b_fs4, ba], writes=[b_dbuf[qc]])
                    sc.op("dve", (lambda e, qc=qc: e.scalar_tensor_tensor(out=fj, in0=dbuf[:, qc, :], scalar=1.0, in1=dbuf[:, qc, :], op0=ALU.mult, op1=ALU.mult,
                                                                           accum_out=fs4[:, 12 + qc:13 + qc])), reads=[b_dbuf[qc], b_fs4], writes=[b_fj, b_fs4])
                sc.op("act", lambda e: e.activation(out=fs4[:, 16:20], in_=fs4[:, 12:16], func=AF.Sqrt, scale=1.0 / 128, bias=epsb),
                      reads=[b_fs4, b_eps], writes=[b_fs4])
                sc.op("dve", lambda e: e.reciprocal(out=fs4[:, 16:20], in_=fs4[:, 16:20]), reads=[b_fs4], writes=[b_fs4])
                for qc in range(4):
                    tile_i = j * 4 + qc
                    sc.op("dve", (lambda e, qc=qc, tile_i=tile_i, h=h: e.scalar_tensor_tensor(
                        out=ATT[:, tile_i, h * 128:(h + 1) * 128], in0=dbuf[:, qc, :], scalar=fs4[:, 16 + qc:17 + qc], in1=subg8[:, :], op0=ALU.mult, op1=ALU.mult)),
                        reads=[b_dbuf[qc], b_fs4, b_subg], writes=[b_ATT[tile_i]])
        if debug & 2:
            final_toks.append(sc.dma("sp", lambda e, s: e.dma_start(out=dbg["att"][:, :], in_=ATT.rearrange("p t e -> p (t e)")).then_inc(s, 16),
                                     sc.buf("dbgatt"), reads=b_ATT))
        sc.barrier()
        if _STOP_AFTER == 2:
            early_finish()

        A.reset(R1_END)
        h2T = A.bf(8 * (TOWN + 8)).rearrange("p (k n) -> p k n", k=8)
        H2_END = A.off
        b_h2T = [sc.buf() for _ in range(16)]
        b_h2h = sc.buf("h2halo")
        woutb = A.bf(8 * D).rearrange("p (k n) -> p k n", k=8)
        b_wout = sc.buf("wout")
        wout_v = wout_d.ap().rearrange("(k p) n -> p k n", p=128)
        sc.dma("pool", lambda e, s: [e.dma_start(out=woutb[:, k, :], in_=wout_v[:, k, :]).then_inc(s, 16) for k in range(8)],
               b_wout, writes=[b_wout], n=8)
        xt = [A.f32(D) for _ in range(2)]
        x1t = [A.f32(D) for _ in range(2)]
        tt_ = [A.f32(D) for _ in range(2)]
        h2tok = [A.bf(D) for _ in range(2)]
        attT = [A.bf(512).rearrange("p (h n) -> p h n", h=4) for _ in range(2)]
        junk = A.f32(D)
        osm = A.f32(32).rearrange("p (a b) -> p a b", a=2)
        WDN_OFF = 33900
        assert A.off <= WDN_OFF, A.off
        A.reset(WDN_OFF)
        wdnb = A.bf(NPAIR * D).rearrange("p (i n) -> p i n", i=NPAIR)
        WDN_END = A.off
        b_wdn = sc.buf("wdn")
        wdn_v = wdn_d.ap().rearrange("(i p) n -> p i n", p=128)
        sc.dma("pool", lambda e, s: [e.dma_start(out=wdnb[:, 2 * i:2 * i + 2, :], in_=wdn_v[:, 2 * i:2 * i + 2, :]).then_inc(s, 16) for i in range(11)],
               b_wdn, writes=[b_wdn], n=11)
        b_xt, b_x1t, b_tt, b_h2tok, b_attT = ([sc.buf(), sc.buf()] for _ in range(5))
        b_junk = sc.buf()
        b_osm = [sc.buf(), sc.buf()]
        x1_toks = []
        hal_toks = []
        halloc_v = halloc.ap()
        def op_stage1(t):
            par = t % 2
            jblk = t // 4
            sc.dma("sp", (lambda e, s, t=t, par=par: e.dma_start(out=xt[par], in_=xtok_d[t * 128:(t + 1) * 128, :]).then_inc(s, 16)),
                   b_xt[par], writes=[b_xt[par]])
            bk = nb()

            def fn_tr(e, t=t, bk=bk):
                ins = None
                for h in range(NH):
                    ins = e.transpose(out=psbf[:, bk * 1024 + h * 128:bk * 1024 + (h + 1) * 128], in_=ATT[:, t, h * 128:(h + 1) * 128], identity=ident)
                return ins
            sc.op("pe", fn_tr, reads=[b_ATT[t], b_cbf], writes=[pbuf[bk]])
            sc.op("dve", (lambda e, bk=bk, par=par: e.tensor_copy(out=attT[par][:, :, :],
                                                                  in_=psbf[:, bk * 1024:bk * 1024 + 512].rearrange("p (h n) -> p h n", h=4))),
                  reads=[pbuf[bk]], writes=[b_attT[par]])
            bks = [nb(), nb()]
            for nh in range(2):
                lhs = [attT[par][:, kc, :] for kc in range(4)] + [CT[:, m, t * 128:(t + 1) * 128] for m in range(4)]
                rhs = [woutb[:, kc, nh * 512:(nh + 1) * 512] for kc in range(8)]
                mm_group(bks[nh], lhs, rhs, 512, reads=[b_attT[par], b_CT[jblk], b_wout])
            o = osm[:, par, :]
            for nh in range(2):
                sc.op("act", (lambda e, nh=nh, o=o, bks=bks: e.activation(out=junk[:, 0:512], in_=bank(bks[nh]), func=AF.Square,
                                                                          accum_out=o[:, nh:nh + 1])),
                      reads=[pbuf[bks[nh]]], writes=[b_junk, b_osm[par]])
            sc.op("dve", (lambda e, o=o: e.tensor_tensor(out=o[:, 2:3], in0=o[:, 0:1], in1=o[:, 1:2], op=ALU.add)), reads=[b_osm[par]], writes=[b_osm[par]])
            sc.op("act", (lambda e, o=o: e.activation(out=o[:, 4:5], in_=o[:, 2:3], func=AF.Sqrt, scale=1.0 / D, bias=epsb)), reads=[b_osm[par], b_eps], writes=[b_osm[par]])
            sc.op("dve", (lambda e, o=o: e.reciprocal(out=o[:, 4:5], in_=o[:, 4:5])), reads=[b_osm[par]], writes=[b_osm[par]])
            for nh in range(2):
                sc.op("dve", (lambda e, nh=nh, o=o, bks=bks, par=par: e.scalar_tensor_tensor(
                    out=tt_[par][:, nh * 512:(nh + 1) * 512], in0=bank(bks[nh]), scalar=o[:, 4:5], in1=rows[:, R_GP1 + nh * 512:R_GP1 + (nh + 1) * 512],
                    op0=ALU.mult, op1=ALU.mult)), reads=[pbuf[bks[nh]], b_osm[par], b_rows], writes=([b_tt[par]] if nh == 0 else []),
                    sets=([b_tt[par]] if nh == 1 else []))
            sc.op("pool", (lambda e, par=par: e.tensor_tensor(out=x1t[par], in0=tt_[par], in1=xt[par], op=ALU.add)),
                  reads=[b_tt[par], b_xt[par]], writes=[b_x1t[par]])
            x1_toks.append(sc.dma("sp", (lambda e, s, t=t, par=par: e.dma_start(out=x1_d[t * 128:(t + 1) * 128, :], in_=x1t[par]).then_inc(s, 16)),
                                  b_x1t[par], reads=[b_x1t[par]]))
            if debug & 4:
                final_toks.append(sc.dma("sp", (lambda e, s, t=t, par=par: e.dma_start(out=dbg["x1"][t * 128:(t + 1) * 128, :], in_=x1t[par]).then_inc(s, 16)),
                                         b_x1t[par], reads=[b_x1t[par]]))
            sc.op("act", (lambda e, par=par, o=o: e.activation(out=junk, in_=x1t[par], func=AF.Square, accum_out=o[:, 5:6])),
                  reads=[b_x1t[par]], writes=[b_junk, b_osm[par]])
            sc.op("act", (lambda e, o=o: e.activation(out=o[:, 7:8], in_=o[:, 5:6], func=AF.Sqrt, scale=1.0 / D, bias=epsb)), reads=[b_osm[par], b_eps], writes=[b_osm[par]])
            sc.op("dve", (lambda e, o=o: e.reciprocal(out=o[:, 7:8], in_=o[:, 7:8])), reads=[b_osm[par]], writes=[b_osm[par]])
            sc.op("dve", (lambda e, par=par, o=o: e.tensor_scalar(out=h2tok[par], in0=x1t[par], scalar1=o[:, 7:8], scalar2=None, op0=ALU.mult)),
                  reads=[b_x1t[par], b_osm[par]], writes=[b_h2tok[par]])
            if t % 4 == 3:
                hal_toks.append(sc.dma("sp", (lambda e, s, par=par, jblk=jblk: e.dma_start(out=halloc_v[2 * jblk:2 * jblk + 2, :],
                                                                                             in_=h2tok[par][126:128, :]).then_inc(s, 16)),
                                       b_h2tok[par], reads=[b_h2tok[par]]))

        def op_stage2(t):
            par = t % 2
            bk2 = nb()

            def fn_tr2(e, par=par, bk2=bk2):
                ins = None
                for kc in range(8):
                    ins = e.transpose(out=psbf[:, bk2 * 1024 + kc * 128:bk2 * 1024 + (kc + 1) * 128], in_=h2tok[par][:, kc * 128:(kc + 1) * 128],
                                      identity=ident)
                return ins
            sc.op("pe", fn_tr2, reads=[b_h2tok[par], b_cbf], writes=[pbuf[bk2]])
            for kc in range(8):
                sc.op("act", (lambda e, kc=kc, t=t, bk2=bk2: e.activation(out=h2T[:, kc, t * 128:(t + 1) * 128],
                                                                         in_=psbf[:, bk2 * 1024 + kc * 128:bk2 * 1024 + (kc + 1) * 128],
                                                                         func=AF.Copy, scale=vecs[:, V_G2 + kc:V_G2 + kc + 1])),
                      reads=[pbuf[bk2], b_vecs], writes=([b_h2T[t]] if kc == 0 else []), sets=([b_h2T[t]] if kc == 7 else []))
        for t in range(17):
            if t < 16:
                op_stage1(t)
            if t >= 1:
                op_stage2(t - 1)
        cc2 = stack.enter_context(nc.semaphore("cc2"))
        sc.sems["cc2"] = cc2
        sc.wait_all("pool", hal_toks)

        def fn_ag2(e):
            ins = e.collective_compute("AllGather", ALU.bypass, replica_groups=[list(range(NCORES))],
                                       ins=[halloc.ap()], outs=[halall.ap()])
            ins.then_inc(cc2, 1)
            return ins
        if not sc.disabled:
            sc.prog["pool"].append(("c", fn_ag2, "cc2"))
        ag2_tok = sc.op("pool", lambda e: e.memset(smalls[:, 7:8], 0.0), writes=[sc.buf("agflag2")])
        sc.barrier()

        if _STOP_AFTER == 3:
            early_finish()
        A.reset(0)
        YT = A.bf(NPAIR * 1024).rearrange("p (i n) -> p i n", i=NPAIR)
        halsb = A.bf(D)
        selsb = A.bf(8)
        assert A.off <= R1_END
        A.reset(max(A.off, H2_END))
        b_hal = sc.buf("halsb")
        b_sel = sc.buf("selsb")
        sc.dma("sp", lambda e, s: e.dma_start(out=halsb[0:64, :], in_=halall.ap()[:, :]).then_inc(s, 16), b_hal, writes=[b_hal], after=[ag2_tok])
        sc.dma("sp", lambda e, s: e.dma_start(out=selsb[0:64, :], in_=sel_d[:, :]).then_inc(s, 16), b_sel, writes=[b_sel])
        bkx = nb()

        def fn_sel(e, bkx=bkx):
            ins = None
            for kc in range(8):
                ins = e.matmul(bank(bkx, 8, kc * 8), lhsT=halsb[0:64, kc * 128:(kc + 1) * 128], rhs=selsb[0:64, :], start=(kc == 0), stop=(kc == 7),
                               skip_group_check=True)
            return ins
        sc.op("pe", fn_sel, reads=[b_hal, b_sel], writes=[pbuf[bkx]])
        for kc in range(8):
            sc.op("act", (lambda e, kc=kc, bkx=bkx: e.activation(out=h2T[:, kc, TOWN:TOWN + 8], in_=bank(bkx, 8, kc * 8), func=AF.Copy,
                                                                 scale=vecs[:, V_G2 + kc:V_G2 + kc + 1])),
                  reads=[pbuf[bkx], b_vecs], writes=([b_h2h] if kc == 0 else []), sets=([b_h2h] if kc == 7 else []))
        NW = 4
        wups = [A.bf(8 * 256).rearrange("p (k n) -> p k n", k=8) for _ in range(NW)]
        b_wup = [sc.buf("wup%d" % i) for i in range(NW)]
        wup_v = wup_d.ap().rearrange("(k p) n -> p k n", p=128)
        NU = 3
        ugu = [A.f32(2 * (BLK + 2)).rearrange("p (g n) -> p g n", g=2) for _ in range(NU)]
        ga_ = [A.f32(BLK) for _ in range(NU)]
        gu_ = [A.f32(BLK) for _ in range(NU)]
        gsq = [A.f32(BLK) for _ in range(2)]
        gp = [A.f32(BLK) for _ in range(2)]
        gsg = [A.f32(BLK) for _ in range(2)]
        b_ugu, b_ga, b_gu = ([sc.buf() for _ in range(NU)] for _ in range(3))
        b_gsq, b_gp, b_gsg = ([sc.buf(), sc.buf()] for _ in range(3))
        assert A.off <= WDN_OFF, A.off
        A.reset(WDN_END)
        x1r = [A.f32(D) for _ in range(2)]
        ost = [A.f32(D) for _ in range(2)]
        fjunk = gsq[0]
        b_fjunk = b_gsq[0]
        fsm = A.f32(16).rearrange("p (a b) -> p a b", a=2)
        b_x1r, b_ost, b_fsm = ([sc.buf(), sc.buf()] for _ in range(3))
        b_YT = [[sc.buf() for _ in range(2)] for _ in range(NPAIR)]
        wctr = [0]
        slot_of_pair = {}

        def fw(ch, k):
            return vecs[:, V_FW + ch * 3 + k:V_FW + ch * 3 + k + 1]

        def fb(ch):
            return vecs[:, V_FB + ch:V_FB + ch + 1]

        def issue_w(hh, i):
            ws = wctr[0] % NW
            wctr[0] += 1
            slot_of_pair[(hh, i)] = ws

            def fn_w(e, s, ws=ws, i=i):
                e.dma_start(out=wups[ws][:, :, 0:128], in_=wup_v[:, :, i * 128:(i + 1) * 128]).then_inc(s, 16)
                e.dma_start(out=wups[ws][:, :, 128:256], in_=wup_v[:, :, DFF + i * 128:DFF + (i + 1) * 128]).then_inc(s, 16)
            sc.dma("pool", fn_w, b_wup[ws], writes=[b_wup[ws]], n=2)

        def stage_A(n, hh, i, bl):
            ws = slot_of_pair[(hh, i)]
            jb = hh * 2 + bl
            u = n % NU
            pr = n % 3
            bkg, bku, bkh2 = 2 * pr, 2 * pr + 1, 6 + (n % 2)
            hreads = [b_h2T[jb * 4 + q] for q in range(4)] + [b_wup[ws]]
            mm_group(bkg, [wups[ws][:, kc, 0:128] for kc in range(8)], [h2T[:, kc, jb * BLK:(jb + 1) * BLK] for kc in range(8)], 512, reads=hreads)
            mm_group(bku, [wups[ws][:, kc, 128:256] for kc in range(8)], [h2T[:, kc, jb * BLK:(jb + 1) * BLK] for kc in range(8)], 512, reads=hreads)

            def fn_hh(e, ws=ws, jb=jb, bkh2=bkh2):
                ins = None
                for g in range(2):
                    for kc in range(8):
                        ins = e.matmul(bank(bkh2, 2, g * 2), lhsT=wups[ws][:, kc, g * 128:(g + 1) * 128],
                                       rhs=h2T[:, kc, TOWN + 2 * jb:TOWN + 2 * jb + 2], start=(kc == 0 and g == 0), stop=(kc == 7),
                                       skip_group_check=True)
                return ins
            sc.op("pe", fn_hh, reads=[b_h2h, b_wup[ws]], writes=[pbuf[bkh2]])
            chg, chu = i, NPAIR + i
            src2 = psum[:, bkg * 512:bkg * 512 + 1024].rearrange("p (g n) -> p g n", g=2)
            sc.op("act", (lambda e, u=u, src2=src2: e.activation(out=ugu[u][:, :, 2:BLK + 2], in_=src2, func=AF.Copy)),
                  reads=[pbuf[bkg], pbuf[bku]], writes=[b_ugu[u]])
            sc.op("act", (lambda e, u=u, bkh2=bkh2: e.activation(out=ugu[u][:, :, 0:2], in_=bank(bkh2, 4).rearrange("p (g n) -> p g n", g=2),
                                                                func=AF.Copy)), reads=[pbuf[bkh2], b_ugu[u]], writes=[b_ugu[u]])
            sc.op("act", (lambda e, u=u, bkg=bkg, chg=chg: e.activation(out=ga_[u], in_=bank(bkg), func=AF.Identity,
                                                                        scale=fw(chg, 2), bias=fb(chg))),
                  reads=[pbuf[bkg], b_vecs], writes=[b_ga[u]])
            sc.op("act", (lambda e, u=u, bku=bku, chu=chu: e.activation(out=gu_[u], in_=bank(bku), func=AF.Identity,
                                                                        scale=fw(chu, 2), bias=fb(chu))),
                  reads=[pbuf[bku], b_vecs], writes=[b_gu[u]])

        def stage_B(n, i):
            u = n % NU
            chg, chu = i, NPAIR + i
            for k in (1, 0):
                sc.op("dve", (lambda e, u=u, k=k, chg=chg: e.scalar_tensor_tensor(out=ga_[u], in0=ugu[u][:, 0, k:k + BLK], scalar=fw(chg, k), in1=ga_[u],
                                                                                  op0=ALU.mult, op1=ALU.add)),
                      reads=[b_ugu[u], b_ga[u]], writes=[b_ga[u]])
                sc.op("dve", (lambda e, u=u, k=k, chu=chu: e.scalar_tensor_tensor(out=gu_[u], in0=ugu[u][:, 1, k:k + BLK], scalar=fw(chu, k), in1=gu_[u],
                                                                                  op0=ALU.mult, op1=ALU.add)),
                      reads=[b_ugu[u], b_gu[u]], writes=[b_gu[u]])

        def stage_C1(n):
            u, v = n % NU, n % 2
            sc.op("act", (lambda e, u=u, v=v: e.activation(out=gsq[v], in_=ga_[u], func=AF.Square)), reads=[b_ga[u]], writes=[b_gsq[v]])

        def stage_C1b(n):
            u, v = n % NU, n % 2
            sc.op("dve", (lambda e, v=v: e.tensor_scalar(out=gp[v], in0=gsq[v], scalar1=0.044715, scalar2=1.0, op0=ALU.mult, op1=ALU.add)),
                  reads=[b_gsq[v]], writes=[b_gp[v]])
            sc.op("dve", (lambda e, u=u, v=v: e.tensor_tensor(out=gp[v], in0=gp[v], in1=ga_[u], op=ALU.mult)), reads=[b_gp[v], b_ga[u]], writes=[b_gp[v]])

        def stage_C2a(n):
            u, v = n % NU, n % 2
            sc.op("act", (lambda e, v=v: e.activation(out=gsg[v], in_=gp[v], func=AF.Sigmoid, scale=1.5957691216057308)),
                  reads=[b_gp[v]], writes=[b_gsg[v]])
            sc.op("pool", (lambda e, u=u, v=v: e.tensor_tensor(out=gsg[v], in0=gsg[v], in1=ga_[u], op=ALU.mult)), reads=[b_gsg[v], b_ga[u]], writes=[b_gsg[v]])

        def stage_C2b(n, i, bl):
            u, v = n % NU, n % 2
            sc.op("dve", (lambda e, u=u, v=v, i=i, bl=bl: e.tensor_tensor(out=YT[:, i, bl * BLK:(bl + 1) * BLK], in0=gsg[v], in1=gu_[u], op=ALU.mult)),
                  reads=[b_gsg[v], b_gu[u]], writes=[b_YT[i][bl]])

        nglob = [0]
        for hh in range(2):
            unit_list = [(i, bl) for i in range(NPAIR) for bl in range(2)]
            NUN = len(unit_list)
            base = nglob[0]
            issue_w(hh, 0)
            issue_w(hh, 1)
            for t in range(NUN + 2):
                if t < NUN:
                    i, bl = unit_list[t]
                    if bl == 0 and i + 2 < NPAIR:
                        issue_w(hh, i + 2)
                    stage_A(base + t, hh, i, bl)
                if 1 <= t <= NUN:
                    stage_B(base + t - 1, unit_list[t - 1][0])
                if 2 <= t:
                    stage_C2a(base + t - 2)
                if 1 <= t <= NUN:
                    stage_C1(base + t - 1)
                if 2 <= t:
                    stage_C2b(base + t - 2, *unit_list[t - 2])
                if 1 <= t <= NUN:
                    stage_C1b(base + t - 1)
            nglob[0] += NUN
            for tl in range(8):
                t = hh * 8 + tl
                par = t % 2
                bl = tl // 4
                sc.dma("sp", (lambda e, s, t=t, par=par: e.dma_start(out=x1r[par], in_=x1_d[t * 128:(t + 1) * 128, :]).then_inc(s, 16)),
                       b_x1r[par], writes=[b_x1r[par]], after=[x1_toks[t]])
                bks = [nb(), nb()]
                for nh in range(2):
                    mm_group(bks[nh], [YT[:, i, tl * 128:(tl + 1) * 128] for i in range(NPAIR)],
                             [wdnb[:, i, nh * 512:(nh + 1) * 512] for i in range(NPAIR)], 512,
                             reads=[b_YT[i][bl] for i in range(NPAIR)] + [b_wdn])
                o = fsm[:, par, :]
                for nh in range(2):
                    sc.op("act", (lambda e, nh=nh, o=o, bks=bks: e.activation(out=fjunk, in_=bank(bks[nh]), func=AF.Square, accum_out=o[:, nh:nh + 1])),
                          reads=[pbuf[bks[nh]]], writes=[b_fjunk, b_fsm[par]])
                sc.op("dve", (lambda e, o=o: e.tensor_tensor(out=o[:, 2:3], in0=o[:, 0:1], in1=o[:, 1:2], op=ALU.add)), reads=[b_fsm[par]], writes=[b_fsm[par]])
                sc.op("act", (lambda e, o=o: e.activation(out=o[:, 4:5], in_=o[:, 2:3], func=AF.Sqrt, scale=1.0 / D, bias=epsb)), reads=[b_fsm[par], b_eps], writes=[b_fsm[par]])
                sc.op("dve", (lambda e, o=o: e.reciprocal(out=o[:, 4:5], in_=o[:, 4:5])), reads=[b_fsm[par]], writes=[b_fsm[par]])
                for nh in range(2):
                    sc.op("dve", (lambda e, nh=nh, o=o, bks=bks, par=par: e.scalar_tensor_tensor(
                        out=ost[par][:, nh * 512:(nh + 1) * 512], in0=bank(bks[nh]), scalar=o[:, 4:5],
                        in1=rows[:, R_GP2 + nh * 512:R_GP2 + (nh + 1) * 512], op0=ALU.mult, op1=ALU.mult)),
                        reads=[pbuf[bks[nh]], b_fsm[par], b_rows], writes=([b_ost[par]] if nh == 0 else []), sets=([b_ost[par]] if nh == 1 else []))
                sc.op("pool", (lambda e, par=par: e.tensor_tensor(out=ost[par], in0=ost[par], in1=x1r[par], op=ALU.add)),
                      reads=[b_ost[par], b_x1r[par]], writes=[b_ost[par]])
                final_toks.append(sc.dma("sp", (lambda e, s, t=t, par=par: e.dma_start(out=out_d[t * 128:(t + 1) * 128, :], in_=ost[par]).then_inc(s, 16)),
                                         b_ost[par], reads=[b_ost[par]]))
        sc.wait_all("sp", final_toks)

        def replay(eng, e):
            for it in sc.prog[eng]:
                if it[0] == "w":
                    e.wait_ge(sc.sems[it[1]], it[2])
                elif it[0] == "o":
                    it[1](e).then_inc(sc.sems[eng], 1)
                elif it[0] == "c":
                    it[1](e)
                    e.wait_ge(sc.sems[it[2]], 1)
                else:
                    it[1](e, sc.sems[it[2]])

        @block.tensor
        def _(e):
            replay("pe", e)

        @block.scalar
        def _(e):
            replay("act", e)

        @block.vector
        def _(e):
            replay("dve", e)

        @block.gpsimd
        def _(e):
            replay("pool", e)

        @block.sync
        def _(e):
            replay("sp", e)
    return nc


def _host_prep(inputs):
    f32 = np.float32
    x = np.asarray(inputs["x"], dtype=f32)[0]
    pos = np.asarray(inputs["positions"]).astype(np.int32)[0]
    w_in = np.asarray(inputs["w_in"], dtype=f32)[0]
    perm = np.arange(512).reshape(4, 2, 2, 32)[:, :, ::-1, :].reshape(512)
    q, k, v, cg = w_in[:, 0:512], w_in[:, 512:1024], w_in[:, 1024:1536], w_in[:, 1536:2560]
    win = np.ascontiguousarray(np.concatenate([q, q[:, perm], k, k[:, perm], v, cg], axis=1))
    wout = np.ascontiguousarray(np.asarray(inputs["w_out"], dtype=f32)[0])
    wup = np.ascontiguousarray(np.asarray(inputs["w_up"], dtype=f32)[0])
    wdn = np.ascontiguousarray(np.asarray(inputs["w_down"], dtype=f32)[0])
    vecs = np.zeros((128, NV), f32)
    vecs[:, V_G1:V_G1 + 8] = np.asarray(inputs["attn_pre_g"], f32)[0].reshape(8, 128).T
    vecs[:, V_G2:V_G2 + 8] = np.asarray(inputs["ffn_pre_g"], f32)[0].reshape(8, 128).T
    cw = np.asarray(inputs["conv_dw_w"], f32)[0]
    vecs[:, V_CW:V_CW + 124] = cw.reshape(31, 4, 128).transpose(2, 1, 0).reshape(128, 124)
    vecs[:, V_CB:V_CB + 4] = np.asarray(inputs["conv_dw_b"], f32)[0].reshape(4, 128).T
    vecs[:, V_LG:V_LG + 4] = np.asarray(inputs["conv_ln_g"], f32)[0].reshape(4, 128).T
    vecs[:, V_LB:V_LB + 4] = np.asarray(inputs["conv_ln_b"], f32)[0].reshape(4, 128).T
    fw = np.asarray(inputs["ffn_dw_w"], f32)[0]
    vecs[:, V_FW:V_FW + 132] = fw.reshape(3, 44, 128).transpose(2, 1, 0).reshape(128, 132)
    vecs[:, V_FB:V_FB + 44] = np.asarray(inputs["ffn_dw_b"], f32)[0].reshape(44, 128).T
    inv_freq = (np.float32(10000.0) ** (-np.arange(0, 64, 2, dtype=np.float32) / np.float32(64))).astype(f32)
    p = np.arange(128)
    vecs[:, V_IF] = inv_freq[p % 32]
    sign = np.where((p // 32) % 2 == 0, -1.0, 1.0).astype(f32)
    vecs[:, V_SG] = sign
    vecs[:, V_NPS] = (-np.float32(PI) * sign).astype(f32)
    rows = np.zeros((1, NR), f32)
    rows[0, R_GP1:R_GP1 + 1024] = np.asarray(inputs["attn_post_g"], f32)[0]
    rows[0, R_GP2:R_GP2 + 1024] = np.asarray(inputs["ffn_post_g"], f32)[0]
    rows[0, R_SUB:R_SUB + 128] = np.asarray(inputs["subln_g"], f32)[0]
    rows[0, R_LQ1:R_LQ1 + 64] = np.asarray(inputs["lambda_q1"], f32)[0]
    rows[0, R_LK1:R_LK1 + 64] = np.asarray(inputs["lambda_k1"], f32)[0]
    rows[0, R_LQ2:R_LQ2 + 64] = np.asarray(inputs["lambda_q2"], f32)[0]
    rows[0, R_LK2:R_LK2 + 64] = np.asarray(inputs["lambda_k2"], f32)[0]
    bf = ml_dtypes.bfloat16
    cbf = np.zeros((128, 256), bf)
    cbf[:, 0:128] = np.eye(128, dtype=f32).astype(bf)
    kk = np.arange(128)[:, None]
    qq = np.arange(512)[None, :]
    diag = np.stack([(qq >= kt * 128 + kk) for kt in range(4)], axis=1).astype(f32)
    ones = np.ones((128, 4, 512), f32)
    zeros = np.zeros((128, 4, 512), f32)
    in_maps = []
    for c in range(NCORES):
        blocks = [8 * j + c for j in range(NOWN)]
        xT = np.zeros((D, NOWN * XW), f32)
        for j, g in enumerate(blocks):
            if g > 0:
                xT[:, j * XW:j * XW + HALO] = x[g * BLK - HALO:g * BLK].T
            xT[:, j * XW + HALO:(j + 1) * XW] = x[g * BLK:(g + 1) * BLK].T
        xtok = np.ascontiguousarray(np.concatenate([x[g * BLK:(g + 1) * BLK] for g in blocks], axis=0))
        posc = np.ascontiguousarray(np.concatenate([pos[g * BLK:(g + 1) * BLK] for g in blocks])[None, :])
        m = np.zeros((NOWN, 8, 128, 4, 512), f32)
        for j in range(NOWN):
            for sl in range(8):
                m[j, sl] = ones if sl < c else (diag if sl == c else zeros)
        masks = m.reshape(NOWN * 8 * 128, 4 * 512).astype(bf)
        sel = np.zeros((64, 8), f32)
        for j in range(NOWN):
            g = 8 * j + c
            if g == 0:
                continue
            pr, pj = ((c - 1), j) if c >= 1 else (7, j - 1)
            for t in range(2):
                sel[pr * 8 + 2 * pj + t, 2 * j + t] = 1.0
        in_maps.append({"xT": xT, "xtok": xtok, "pos": posc, "win": win, "wout": wout, "wup": wup, "wdn": wdn,
                        "vecs": vecs, "rows": rows, "cbf": cbf, "masks": masks, "sel": sel.astype(bf)})
    return in_maps


_NC_CACHE = {}


def kernel(**inputs):
    in_maps = _host_prep(inputs)
    if "nc" not in _NC_CACHE:
        _NC_CACHE["nc"] = build(debug=0)
    nc = _NC_CACHE["nc"]
    res = run_bass_kernel_spmd(nc, in_maps, core_ids=list(range(NCORES)))
    out = np.zeros((S, D), np.float32)
    for c in range(NCORES):
        o = np.asarray(res.results[c]["out"], dtype=np.float32)
        for j in range(NOWN):
            g = 8 * j + c
            out[g * BLK:(g + 1) * BLK] = o[j * BLK:(j + 1) * BLK]
    return out[None]
```
